# Optimizing a Trainium2 kernel written in Bass

```python
import jax, jax.numpy as jnp
from jax import lax
import numpy as np

D_MODEL = 1024
BATCH = 8
SEQ = 2048
DEPTH = 4
DEC_BATCH = 128
DEC_SEQ = 8
PAST_LEN = 16384
PAGE_SIZE = 128

POOL_WIDTH = D_MODEL // 2
POOL_WINDOWS = (2, 4, 8, 16)
N_POOL_GROUPS = len(POOL_WINDOWS)
POOL_GROUP_DIM = POOL_WIDTH // N_POOL_GROUPS
POOL_BUF = max(POOL_WINDOWS) - 1
SG_WIDTH = D_MODEL - POOL_WIDTH
SG_HEADS = 4
SG_HEAD_DIM = SG_WIDTH // SG_HEADS
CHUNK = 128
MIX_WIDTH = POOL_WIDTH + SG_WIDTH
IN_WIDTH = POOL_WIDTH + 2 * SG_WIDTH
D_FF = ((8 * D_MODEL // 3 + 127) // 128) * 128
CONV_WIDTH = 3
CONV_BUF = CONV_WIDTH - 1
EPS = 1e-6

kernel_name = "hybrid_pool_sgmlp_convffn_step"


def rmsnorm(x, g):
    x32 = x.astype(jnp.float32)
    y = x32 * lax.rsqrt(jnp.mean(x32 * x32, axis=-1, keepdims=True) + EPS)
    return (y * g.astype(jnp.float32)).astype(x.dtype)


def pool_mixer(p, past, pos0, w_pool, scale):
    B, T, _ = p.shape
    full = jnp.concatenate([past.astype(p.dtype), p], axis=1)
    csum = jnp.cumsum(full.astype(jnp.float32), axis=1)
    c = jnp.concatenate([jnp.zeros((B, 1, POOL_WIDTH), jnp.float32), csum], axis=1)
    pos = pos0 + jnp.arange(T)
    end = c[:, POOL_BUF + 1:POOL_BUF + 1 + T]
    means = []
    for gi, w in enumerate(POOL_WINDOWS):
        sl = slice(gi * POOL_GROUP_DIM, (gi + 1) * POOL_GROUP_DIM)
        start = c[:, POOL_BUF + 1 - w:POOL_BUF + 1 - w + T, sl]
        cnt = jnp.minimum(pos + 1, w).astype(jnp.float32)[None, :, None]
        means.append((end[..., sl] - start) / cnt)
    d = jnp.concatenate(means, axis=-1) - p.astype(jnp.float32)
    d = d.reshape(B, T, N_POOL_GROUPS, POOL_GROUP_DIM)
    out = jnp.einsum('btgc,gcd->btgd', d, w_pool.astype(jnp.float32)).reshape(B, T, POOL_WIDTH)
    out = out * scale.astype(jnp.float32)
    return out.astype(p.dtype), full[:, -POOL_BUF:]


def spatial_gate(u, v, w_s, b_s):
    B, T, _ = u.shape
    L = min(T, CHUNK)
    nc = T // L
    u = u.reshape(B, nc, L, SG_HEADS, SG_HEAD_DIM)
    v = v.reshape(B, nc, L, SG_HEADS, SG_HEAD_DIM)
    mask = jnp.tril(jnp.ones((L, L), dtype=bool))
    w = jnp.where(mask[None], w_s[:, :L, :L], jnp.zeros((), w_s.dtype))
    mixed = jnp.einsum('hts,bcshd->bcthd', w, v) + b_s[:, :L].T[None, None, :, :, None]
    return (u * mixed).reshape(B, T, SG_WIDTH)


def conv_ffn(h, past, w_gate, w_up, conv_w, conv_b, w_down):
    T = h.shape[1]
    g = h @ w_gate
    full = jnp.concatenate([past.astype(g.dtype), g], axis=1)
    conv = conv_b + sum(conv_w[k] * full[:, k:k + T] for k in range(CONV_WIDTH))
    y = (jax.nn.gelu(conv, approximate=False) * (h @ w_up)) @ w_down
    return y, full[:, -CONV_BUF:]


def layer(x, pool_past, conv_past, pos0, norm1_g, w_in, pool_w, pool_scale, v_norm_g,
          w_spatial, b_spatial, w_out, norm2_g, w_gate, w_up, conv_w, conv_b, w_down):
    h = rmsnorm(x, norm1_g)
    z = h @ w_in
    p = z[..., :POOL_WIDTH]
    uv = jax.nn.gelu(z[..., POOL_WIDTH:], approximate=False)
    u = uv[..., :SG_WIDTH]
    v = rmsnorm(uv[..., SG_WIDTH:], v_norm_g)
    a, pool_state = pool_mixer(p, pool_past, pos0, pool_w, pool_scale)
    s = spatial_gate(u, v, w_spatial, b_spatial)
    x = x + jnp.concatenate([a, s], axis=-1) @ w_out
    y, conv_state = conv_ffn(rmsnorm(x, norm2_g), conv_past, w_gate, w_up, conv_w, conv_b, w_down)
    return x + y, pool_state, conv_state, v


def setup_inputs(seed: int = 0) -> dict:
    key = jax.random.key(seed)
    ks = jax.random.split(key, 20)
    f32 = jnp.float32
    nrm = lambda k, shape, s: (jax.random.normal(k, shape, f32) * s)
    return {
        "x_prompt": nrm(ks[0], (BATCH, SEQ, D_MODEL), 1.0),
        "x_sample": nrm(ks[1], (DEC_BATCH, DEC_SEQ, D_MODEL), 1.0),
        "state_pool": nrm(ks[2], (DEPTH, DEC_BATCH, POOL_BUF, POOL_WIDTH), 1.0),
        "state_conv": nrm(ks[3], (DEPTH, DEC_BATCH, CONV_BUF, D_FF), 1.0),
        "norm1_g": 1.0 + nrm(ks[4], (DEPTH, D_MODEL), 0.05),
        "w_in": nrm(ks[5], (DEPTH, D_MODEL, IN_WIDTH), D_MODEL ** -0.5),
        "pool_w": nrm(ks[6], (DEPTH, N_POOL_GROUPS, POOL_GROUP_DIM, POOL_GROUP_DIM), POOL_GROUP_DIM ** -0.5),
        "pool_scale": 1.0 + nrm(ks[7], (DEPTH, POOL_WIDTH), 0.1),
        "v_norm_g": 1.0 + nrm(ks[8], (DEPTH, SG_WIDTH), 0.05),
        "w_spatial": nrm(ks[9], (DEPTH, SG_HEADS, CHUNK, CHUNK), 0.5 * CHUNK ** -0.5),
        "b_spatial": 1.0 + nrm(ks[10], (DEPTH, SG_HEADS, CHUNK), 0.1),
        "w_out": nrm(ks[11], (DEPTH, MIX_WIDTH, D_MODEL), MIX_WIDTH ** -0.5),
        "norm2_g": 1.0 + nrm(ks[12], (DEPTH, D_MODEL), 0.05),
        "w_gate": nrm(ks[13], (DEPTH, D_MODEL, D_FF), D_MODEL ** -0.5),
        "w_up": nrm(ks[14], (DEPTH, D_MODEL, D_FF), D_MODEL ** -0.5),
        "conv_w": nrm(ks[15], (DEPTH, CONV_WIDTH, D_FF), 0.5),
        "conv_b": nrm(ks[16], (DEPTH, D_FF), 0.01),
        "w_down": nrm(ks[17], (DEPTH, D_FF, D_MODEL), D_FF ** -0.5),
        "final_norm_g": 1.0 + nrm(ks[18], (D_MODEL,), 0.05),
    }


def reference(x_prompt, x_sample, state_pool, state_conv, norm1_g, w_in, pool_w, pool_scale,
              v_norm_g, w_spatial, b_spatial, w_out, norm2_g, w_gate, w_up, conv_w, conv_b,
              w_down, final_norm_g):
    xp, xs = x_prompt, x_sample
    zero_pool = jnp.zeros((BATCH, POOL_BUF, POOL_WIDTH), x_prompt.dtype)
    zero_conv = jnp.zeros((BATCH, CONV_BUF, D_FF), x_prompt.dtype)
    pool_p, pool_s, conv_p, conv_s, v_s = [], [], [], [], []
    for i in range(DEPTH):
        params = (norm1_g[i], w_in[i], pool_w[i], pool_scale[i], v_norm_g[i], w_spatial[i],
                  b_spatial[i], w_out[i], norm2_g[i], w_gate[i], w_up[i], conv_w[i], conv_b[i], w_down[i])
        xp, ps, cs, _ = layer(xp, zero_pool, zero_conv, 0, *params)
        pool_p.append(ps)
        conv_p.append(cs)
        xs, ps, cs, vs = layer(xs, state_pool[i], state_conv[i], PAST_LEN, *params)
        pool_s.append(ps)
        conv_s.append(cs)
        v_s.append(vs)
    y_prompt = rmsnorm(xp, final_norm_g)
    y_sample = rmsnorm(xs, final_norm_g)
    return (y_prompt, y_sample, jnp.stack(pool_p), jnp.stack(pool_s), jnp.stack(conv_p),
            jnp.stack(conv_s), jnp.stack(v_s))
```

```python
import numpy as np
from contextlib import ExitStack
import concourse.bass as bass
import concourse.mybir as mybir
from concourse.bass_utils import run_bass_kernel_spmd

F32 = mybir.dt.float32
BF16 = mybir.dt.bfloat16
AF = mybir.ActivationFunctionType
ALU = mybir.AluOpType

L = 4
D = 1024
KD = 8
WIN = 1536
FF = 2816
NCH = 22
G = 2
GROUPS = [(c0, min(G, NCH - c0)) for c0 in range(0, NCH, G)]
NG = len(GROUPS)
WINS = (2, 4, 8, 16)
EPS = 1e-6
NSLOT = 9
DBG = {"halves": (0, 1), "layers": L, "mixer": True, "ffn": True, "final": True, "stage": 99, "ntiles": 99, "tables": True}
STRICT = True


class Buf:
    __slots__ = ("name", "w", "r", "psum")

    def __init__(self, name, psum=False):
        self.name = name
        self.w = None
        self.r = {}
        self.psum = psum


class Prog:
    ENG = ("pe", "act", "dve", "pool", "sp")

    def __init__(self, nc):
        self.nc = nc
        self.stream = {k: [] for k in self.ENG}
        self.semh = {}
        self.cnt = {}
        self.known = {k: {} for k in self.ENG}
        for k in ("pe", "act", "dve", "pool"):
            self._sem(k)
        self.out_events = []

    def _sem(self, key):
        if key not in self.semh:
            self.semh[key] = self.nc.alloc_semaphore("s_" + key)
            self.cnt[key] = 0
        return self.semh[key]

    def _deps(self, e, reads, writes):
        need = {}

        def add(key, val, src, same_ok):
            if src == e and same_ok:
                return
            if need.get(key, 0) < val:
                need[key] = val
        for b in reads:
            if b.w is not None:
                add(b.w[0], b.w[1], b.w[2], e == "pe")
            if b.psum:
                for key, (val, src) in b.r.items():
                    if src != e:
                        add(key, val, src, False)
        for b in writes:
            if b.w is not None:
                add(b.w[0], b.w[1], b.w[2], e == "pe" or not STRICT)
            for key, (val, src) in b.r.items():
                add(key, val, src, e == "pe" or not STRICT)
        out = []
        kn = self.known[e]
        for key, val in need.items():
            if kn.get(key, 0) < val:
                kn[key] = val
                out.append((key, val))
        return out

    def _record(self, ev, reads, writes):
        key, val, src = ev
        for b in reads:
            b.r[key] = (val, src)
        for b in writes:
            b.w = ev
            b.r = {}

    def op(self, e, fn, reads=(), writes=()):
        waits = self._deps(e, reads, writes)
        self.cnt[e] += 1
        ev = (e, self.cnt[e], e)
        self.stream[e].append((waits, fn, (e, 1)))
        self._record(ev, reads, writes)
        return ev

    def dma(self, q, semkey, items, reads=(), writes=(), is_output=False, slow=False):
        self._sem(semkey)
        waits = self._deps(q, reads, writes)
        self.cnt[semkey] += 16 * len(items)
        ev = (semkey, self.cnt[semkey], "dma")

        def fn(eng, items=items):
            if slow:
                return [eng.dma_start(out=o, in_=i, allow_slow_non_contiguous=True) for (o, i) in items]
            return [eng.dma_start(out=o, in_=i) for (o, i) in items]
        self.stream[q].append((waits, fn, (semkey, 16)))
        self._record(ev, reads, writes)
        if is_output:
            self.out_events.append(ev)
        return ev

    def finalize(self):
        fin = {}
        for key, val, _ in self.out_events:
            fin[key] = max(fin.get(key, 0), val)
        self.stream["sp"].append(([(k, v) for k, v in fin.items()], None, None))
        nc = self.nc
        with nc.Block() as block:
            decos = {"pe": block.tensor, "act": block.scalar, "dve": block.vector,
                     "pool": block.gpsimd, "sp": block.sync}
            for e in self.ENG:
                def body(eng, e=e):
                    for waits, fn, inc in self.stream[e]:
                        for key, val in waits:
                            eng.wait_ge(self.semh[key], val)
                        if fn is None:
                            continue
                        r = fn(eng)
                        if isinstance(r, (list, tuple)):
                            for ins in r:
                                ins.then_inc(self.semh[inc[0]], inc[1])
                        else:
                            r.then_inc(self.semh[inc[0]], inc[1])
                decos[e](body)


def build_program():
    nc = bass.Bass("TRN2", target_bir_lowering=False)

    def din(n, s):
        return nc.dram_tensor(n, list(s), F32, kind="ExternalInput").ap()

    def dout(n, s):
        return nc.dram_tensor(n, list(s), F32, kind="ExternalOutput").ap()

    x_all = din("x_all", (2176, D))
    sp_d = din("sp", (L, 240, 512))
    sc_d = din("sc", (L, 32, FF))
    w_in = din("wi_h", (L, 128, KD * WIN))
    w_out = din("wo_h", (L, 128, KD * D))
    wgu_d = din("wgu_h", (L, NG, 128, 2 * G * 1024))
    wd_d = din("wd_h", (L, NG, 128, G * 1024))
    pool_w = din("pw_h", (L, 128, 512))
    wsT_d = din("wsT", (L, 128, 512))
    wsS_d = din("wsS", (L, 128, 512))
    g1_d = din("g1", (L, D))
    g2_d = din("g2", (L, D))
    gf_d = din("gf", (1, D))
    vg_d = din("vg", (L, 512))
    bsP_d = din("bsP", (L, 512))
    bsS_d = din("bsS", (L, 512))
    psc_d = din("pscT", (128, L * 4))
    cw_d = din("cwT", (128, L * 3 * NCH))
    cb_d = din("cbT", (128, L * NCH))
    identf_d = din("identf", (128, 128))
    maskP_d = din("maskP", (128, 128))
    maskS_d = din("maskS", (128, 128))
    bands_d = din("bands", (128, 24, 128))
    rc1_d = din("rc1", (1, 512))

    y_all = dout("y_all", (2176, D))
    npp_o = dout("npp", (L, 15, 512))
    nps_o = dout("nps", (L, 16, 15, 512))
    ncp_o = dout("ncp", (L, 2, FF))
    ncs_o = dout("ncs", (L, 32, FF))
    nvs_o = dout("nvs", (L, 128, 512))

    st = ExitStack()
    with st:
        def SB(n, s, d):
            return st.enter_context(nc.sbuf_tensor("sb_" + n, list(s), d))

        pg = Prog(nc)
        xs = SB("xs", (128, NSLOT, D), F32)
        xB = [Buf(f"x{i}") for i in range(NSLOT)]
        h2T = SB("h2T", (128, KD, NSLOT * 128), BF16)
        h2B = [Buf(f"h2_{i}") for i in range(NSLOT)]
        wi = SB("wi", (128, KD, WIN), BF16); wiB = [Buf(f"wi{k}") for k in range(KD)]
        wo = SB("wo", (128, KD, D), BF16); woB = [Buf(f"wo{k}") for k in range(KD)]
        pw = SB("pw", (128, 4, 128), BF16); pwB = Buf("pw")
        wsP = SB("wsP", (128, 4, 128), BF16); wsPB = Buf("wsP")
        wsS = SB("wsS", (128, 4, 128), BF16); wsSB = Buf("wsS")
        wsraw = SB("wsraw", (128, 2, 512), F32); wsrawB = Buf("wsraw")
        wsl = [SB(f"wsl{i}", (128, 2 * G * 1024), BF16) for i in range(2)]
        wgs = [t[:, 0:G * 1024].rearrange("p (k n) -> p k n", k=KD) for t in wsl]
        wus = [t[:, G * 1024:2 * G * 1024].rearrange("p (k n) -> p k n", k=KD) for t in wsl]
        wdl = [SB(f"wdl{i}", (128, G * 1024), BF16) for i in range(3)]
        wds = [t[:].rearrange("p (g n) -> p g n", g=G) for t in wdl]
        fdB = [Buf(f"wdslot{i}") for i in range(3)]
        fsB = [[Buf(f"ffnslot{i}g"), Buf(f"ffnslot{i}u")] for i in range(2)]
        identb = SB("identb", (128, 128), BF16); identbB = Buf("identb")
        identf = SB("identf_s", (128, 128), F32); identfB = Buf("identf")
        maskP = SB("maskP_s", (128, 128), F32); maskS = SB("maskS_s", (128, 128), F32); maskB = Buf("masks")
        bands = SB("bands_s", (128, 24, 128), BF16); bandsB = Buf("bands")
        rc1 = SB("rc1_s", (128, 4, 128), F32); rc1B = Buf("rc1")
        g1bc = SB("g1bc", (128, D), F32); g1B = Buf("g1bc")
        g2bc = SB("g2bc", (128, D), F32); g2B = Buf("g2bc")
        vgbc = SB("vgbc", (128, 512), F32); vgB = Buf("vgbc")
        bsP = SB("bsP_s", (128, 512), F32); bsPB = Buf("bsP")
        bsS = SB("bsS_s", (128, 512), F32); bsSB = Buf("bsS")
        psc = SB("psc", (128, L * 4), F32)
        cw = SB("cw", (128, L * 3 * NCH), F32)
        cb = SB("cb", (128, L * NCH), F32)
        parB = Buf("params")
        stats = SB("stats", (128, 64), F32)
        statB = [Buf(f"stat{i}") for i in range(64)]
        cst_t = SB("cconst", (128, 4), F32); cstB = Buf("cconst")
        pbf = [SB(f"pbf{i}", (128, 512), BF16) for i in range(3)]
        pbfB = [Buf(f"pbf{i}") for i in range(3)]
        pcar = SB("pcar", (128, L, 512), BF16); pcarB = [Buf(f"pcar{i}") for i in range(L)]
        spast = SB("spast", (128, 2, 512), BF16); spastB = Buf("spast")
        gcar = SB("gcar", (128, L, NCH, 4), F32); gcarB = [Buf(f"gcar{i}") for i in range(L)]
        scs = [SB(f"scs{i}", (32, G * 128), F32) for i in range(2)]; scsB = [Buf(f"scs{i}") for i in range(2)]
        cso = [SB(f"cso{i}", (32, G * 128), F32) for i in range(2)]; csoB = [Buf(f"cso{i}") for i in range(2)]
        csl = [SB(f"csl{i}", (128, G, 32), F32) for i in range(2)]; cslB = [Buf(f"csl{i}") for i in range(2)]
        S = [SB(f"S{i}", (128, 514), F32) for i in range(8)]
        SBf = [Buf(f"S{i}") for i in range(8)]
        H = [SB(f"H{i}", (128, 1024), BF16) for i in range(9)]
        HB = [Buf(f"H{i}") for i in range(9)]
        JK = [H[0]] + [SB(f"JK{i}", (128, 1024), BF16) for i in range(2)]
        JKB = [HB[0]] + [Buf(f"JK{i}") for i in range(2)]
        jk_i = [0]
        H7B = [Buf("H7a"), Buf("H7b")]
        H8B = [Buf("H8a"), Buf("H8b")]
        ps = [st.enter_context(nc.psum_tensor(f"ps{i}", [128, 512], F32)) for i in range(8)]
        psB = [Buf(f"ps{i}", psum=True) for i in range(8)]
        ps0b = ps[0][:].bitcast(BF16).rearrange("p (k n) -> p k n", k=8)

        stat_i = [0]

        def new_stat():
            i = stat_i[0] % 64
            stat_i[0] += 1
            return stats[:, i:i + 1], statB[i]

        pg.dma("sp", "cst", [(identf[:], identf_d[:, :]), (maskP[:], maskP_d[:, :]), (maskS[:], maskS_d[:, :]),
                             (rc1[:].rearrange("p g t -> p (g t)"), rc1_d[0:1, :].partition_broadcast(128)),
                             (psc[:], psc_d[:, :]), (cw[:], cw_d[:, :]), (cb[:], cb_d[:, :])],
               writes=[identfB, maskB, rc1B, parB])
        pg.dma("pool", "cstb", [(bands[:], bands_d[:, :, :])], writes=[bandsB])
        pg.op("dve", lambda e: e.tensor_copy(out=identb[:], in_=identf[:]), reads=[identfB], writes=[identbB])
        pg.op("dve", lambda e: e.memset(cst_t[:, 0:1], EPS), writes=[cstB])
        pg.op("dve", lambda e: e.memset(cst_t[:, 1:2], -0.5), writes=[cstB])
        pg.op("dve", lambda e: e.memset(cst_t[:, 2:4], 0.0), writes=[cstB])
        pg.op("dve", lambda e: e.memset(spast[:], 0.0), writes=[spastB])

        ffn_items = [(h, l, j) for h in range(2) for l in range(L) for j in range(NG)]

        dq_u, dq_b = [], []

        def pump(nu=1, nb=1):
            for _ in range(nu):
                if dq_u:
                    dq_u.pop(0)[1]()
            for _ in range(nb):
                if dq_b:
                    dq_b.pop(0)()

        def flush_urgent(upto):
            while dq_u and dq_u[0][0] <= upto:
                dq_u.pop(0)[1]()

        def flush_bg():
            while dq_b:
                dq_b.pop(0)()

        def emit_ffn_load(n):
            if n >= len(ffn_items):
                return
            h, l, j = ffn_items[n]
            s = n % 2
            for hf in range(2):
                dq_u.append((n, lambda s=s, l=l, j=j, hf=hf: pg.dma(
                    "pool", f"ffn{s}{hf}", [(wsl[s][:, hf * G * 1024:(hf + 1) * G * 1024], wgu_d[l, j][:, hf * G * 1024:(hf + 1) * G * 1024])],
                    writes=[fsB[s][hf]])))

        def emit_wd_load(n):
            if n >= len(ffn_items):
                return
            h, l, j = ffn_items[n]
            s = n % 3
            dq_u.append((n, lambda s=s, l=l, j=j: pg.dma("pool", f"ffd{s}", [(wdl[s][:], wd_d[l, j])], writes=[fdB[s]])))

        def emit_mixer_load(l):
            for k in range(KD):
                dq_b.append(lambda k=k, l=l: pg.dma("pool", f"wi{k}", [(wi[:, k, :], w_in[l][:, k * WIN:(k + 1) * WIN])], writes=[wiB[k]]))
            for k in range(KD):
                dq_b.append(lambda k=k, l=l: pg.dma("pool", f"wo{k}", [(wo[:, k, :], w_out[l][:, k * D:(k + 1) * D])], writes=[woB[k]]))
            dq_b.append(lambda l=l: pg.dma("pool", "pw", [(pw[:].rearrange("p g d -> p (g d)"), pool_w[l])], writes=[pwB]))

        def emit_layer_tables(l, half, with_g1=True):
            if with_g1:
                pg.dma("sp", "tab", [(g1bc[:], g1_d[l:l + 1, :].partition_broadcast(128))], writes=[g1B])
            pg.dma("sp", "tab2", [(g2bc[:], g2_d[l:l + 1, :].partition_broadcast(128)),
                                  (vgbc[:], vg_d[l:l + 1, :].partition_broadcast(128)),
                                  (bsP[:], bsP_d[l:l + 1, :].partition_broadcast(128)),
                                  (bsS[:], bsS_d[l:l + 1, :].partition_broadcast(128))],
                   writes=[g2B, vgB, bsPB, bsSB])
            pg.dma("sp", "wsr", [(wsraw[:, 0, :], wsT_d[l]), (wsraw[:, 1, :], wsS_d[l])], writes=[wsrawB])
            if half == 0:
                pg.dma("pool", "spast", [(spast[0:120, :, :], sp_d[l].rearrange("(a r) c -> r a c", a=2))],
                       writes=[spastB])
                pg.dma("sp", "o_nps_past", [(nps_o[l][:, 0:7, :], sp_d[l].rearrange("(q j) c -> q j c", j=15)[:, 8:15, :])],
                       is_output=True)

        def emit_ws_mask(half):
            for hh_ in range(4):
                pg.op("dve", lambda e, hh_=hh_: e.tensor_tensor(out=wsP[:, hh_, :], in0=wsraw[:, 0, hh_ * 128:(hh_ + 1) * 128],
                                                                in1=maskP[:], op=ALU.mult),
                      reads=[wsrawB, maskB], writes=[wsPB])
            if half == 0:
                for hh_ in range(4):
                    pg.op("dve", lambda e, hh_=hh_: e.tensor_tensor(out=wsS[:, hh_, :], in0=wsraw[:, 1, hh_ * 128:(hh_ + 1) * 128],
                                                                    in1=maskS[:], op=ALU.mult),
                          reads=[wsrawB, maskB], writes=[wsSB])

        ring = {"xn": 0, "hTm": 0, "mixT": 0, "pbf": 0}

        def rms_stats(x_ap, xbuf, width):
            ms, msB = new_stat()
            rs, rsB = new_stat()
            ji = jk_i[0] % 3
            jk_i[0] += 1
            pg.op("act", lambda e: e.activation(out=JK[ji][:, 0:width], in_=x_ap, func=AF.Square,
                                                scale=float(width) ** -0.5, accum_out=ms),
                  reads=[xbuf], writes=[JKB[ji], msB])
            pg.op("pool", lambda e: e.tensor_tensor(out=rs, in0=ms, in1=cst_t[:, 0:1], op=ALU.add),
                  reads=[msB, cstB], writes=[rsB])
            pg.op("pool", lambda e: e.tensor_tensor(out=rs, in0=rs, in1=cst_t[:, 1:2], op=ALU.pow),
                  reads=[rsB, cstB], writes=[rsB])
            return rs, rsB

        def norm_to_T(slot, gtab, gB, out_ap, outB):
            x_ap = xs[:, slot, :]
            rs, rsB = rms_stats(x_ap, xB[slot], D)
            i = 1 + ring["xn"] % 2
            ring["xn"] += 1
            xn, xnB = H[i], HB[i]
            pg.op("dve", lambda e: e.scalar_tensor_tensor(out=xn[:], in0=x_ap, scalar=rs, in1=gtab[:],
                                                          op0=ALU.mult, op1=ALU.mult),
                  reads=[xB[slot], rsB, gB], writes=[xnB])

            def tr(e):
                for k in range(KD):
                    ins = e.transpose(ps0b[:, k, :], xn[:, k * 128:(k + 1) * 128], identb[:])
                return ins
            pg.op("pe", tr, reads=[xnB, identbB], writes=[psB[0]])
            pg.op("act", lambda e: e.activation(out=out_ap, in_=ps0b, func=AF.Copy), reads=[psB[0]], writes=[outB])


        ps7b = ps[7][:].bitcast(BF16).rearrange("p (k n) -> p k n", k=KD)
        mstate = {"pbf": 0}

        class TC:
            pass

        def mixer_phase(half, l, tiles):
            n = len(tiles)
            C = []
            for idx, (slot, kind, ptile) in enumerate(tiles):
                c = TC()
                c.slot, c.kind, c.ptile, c.is_s, c.par = slot, kind, ptile, kind == "s", idx % 2
                c.hT = H[3 + c.par][:].rearrange("p (k n) -> p k n", k=KD)
                c.hTB = HB[3 + c.par]
                if (not c.is_s) and ptile == 7:
                    c.pcur, c.pcurB = pcar[:, l, :], pcarB[l]
                else:
                    r = mstate["pbf"] % 3
                    mstate["pbf"] += 1
                    c.pcur, c.pcurB = pbf[r][:], pbfB[r]
                if c.is_s or ptile == 0:
                    c.pprev, c.pprevB = None, None
                elif ptile == 8:
                    c.pprev, c.pprevB = pcar[:, l, :], pcarB[l]
                else:
                    c.pprev, c.pprevB = C[idx - 1].pcur, C[idx - 1].pcurB
                c.gv, c.gvB = S[c.par], SBf[c.par]
                c.tmp, c.tmpB = S[2 + c.par], SBf[2 + c.par]
                c.uT, c.uTB = S[4 + c.par], SBf[4 + c.par]
                c.vnb, c.vnbB = H[7][:, c.par * 512:(c.par + 1) * 512], H7B[c.par]
                c.dTb, c.dTbB = H[8][:, c.par * 512:(c.par + 1) * 512], H8B[c.par]
                c.mixF, c.mixT, c.mixTB = H[5 + c.par], H[5 + c.par][:].rearrange("p (k n) -> p k n", k=KD), HB[5 + c.par]
                C.append(c)

            def norm_a(c, gtab, gB, xn, xnB):
                x_ap = xs[:, c.slot, :]
                rs, rsB = rms_stats(x_ap, xB[c.slot], D)
                pg.op("dve", lambda e: e.scalar_tensor_tensor(out=xn[:], in0=x_ap, scalar=rs, in1=gtab[:],
                                                              op0=ALU.mult, op1=ALU.mult),
                      reads=[xB[c.slot], rsB, gB], writes=[xnB])

            def norm_b(xn, xnB, pst, pstB, out_ap, outB):
                def tr(e):
                    for k in range(KD):
                        ins = e.transpose(pst[:, k, :], xn[:, k * 128:(k + 1) * 128], identb[:])
                    return ins
                pg.op("pe", tr, reads=[xnB, identbB], writes=[pstB])
                pg.op("act", lambda e: e.activation(out=out_ap, in_=pst, func=AF.Copy), reads=[pstB], writes=[outB])

            def N1a(c):
                norm_a(c, g1bc, g1B, H[1], HB[1])

            def N1b(c):
                norm_b(H[1], HB[1], ps0b, psB[0], c.hT, c.hTB)

            def N2a(c):
                norm_a(c, g2bc, g2B, H[2], HB[2])

            def N2b(c):
                norm_b(H[2], HB[2], ps7b, psB[7], h2T[:, :, c.slot * 128:(c.slot + 1) * 128], h2B[c.slot])

            def A2pv(c):
                hT = c.hT

                def inproj(e):
                    for k in range(KD):
                        e.matmul(ps[1][:], lhsT=hT[:, k, :], rhs=wi[:, k, 0:512], start=(k == 0), stop=(k == KD - 1))
                        ins = e.matmul(ps[2][:], lhsT=hT[:, k, :], rhs=wi[:, k, 1024:1536], start=(k == 0), stop=(k == KD - 1))
                    return ins
                pg.op("pe", inproj, reads=[c.hTB] + wiB, writes=[psB[1], psB[2]])

            def A3pv(c):
                pcur, gv, vnb = c.pcur, c.gv, c.vnb
                pg.op("act", lambda e: e.activation(out=pcur, in_=ps[1][:], func=AF.Copy), reads=[psB[1]], writes=[c.pcurB])
                if c.is_s or c.ptile == 15:
                    pg.op("act", lambda e: e.activation(out=S[6][:, 0:512], in_=ps[1][:], func=AF.Copy),
                          reads=[psB[1]], writes=[SBf[6]])
                    if c.is_s:
                        pg.dma("sp", "o_nps", [(nps_o[l][q, 7:15, :], S[6][q * 8:(q + 1) * 8, 0:512]) for q in range(16)],
                               reads=[SBf[6]], is_output=True)
                    else:
                        pg.dma("sp", "o_npp", [(npp_o[l], S[6][113:128, 0:512])], reads=[SBf[6]], is_output=True)
                pg.op("act", lambda e: e.activation(out=gv[:, 0:512], in_=ps[2][:], func=AF.Gelu), reads=[psB[2]], writes=[c.gvB])
                rs, rsB = rms_stats(gv[:, 0:512], c.gvB, 512)
                if c.is_s:
                    pg.op("dve", lambda e: e.scalar_tensor_tensor(out=S[7][:, 0:512], in0=gv[:, 0:512], scalar=rs, in1=vgbc[:],
                                                                  op0=ALU.mult, op1=ALU.mult),
                          reads=[c.gvB, rsB, vgB], writes=[SBf[7]])
                    pg.op("dve", lambda e: e.tensor_copy(out=vnb, in_=S[7][:, 0:512]), reads=[SBf[7]], writes=[c.vnbB])
                    pg.dma("sp", "o_nvs", [(nvs_o[l], S[7][:, 0:512])], reads=[SBf[7]], is_output=True)
                else:
                    pg.op("dve", lambda e: e.scalar_tensor_tensor(out=vnb, in0=gv[:, 0:512], scalar=rs, in1=vgbc[:],
                                                                  op0=ALU.mult, op1=ALU.mult),
                          reads=[c.gvB, rsB, vgB], writes=[c.vnbB])

            def A2u(c):
                hT, uT = c.hT, c.uT

                def inproj_u(e):
                    for j in range(4):
                        for k in range(KD):
                            ins = e.matmul(ps[3][:, j * 128:(j + 1) * 128], lhsT=wi[:, k, 512 + j * 128:512 + (j + 1) * 128],
                                           rhs=hT[:, k, :], start=(k == 0), stop=(k == KD - 1))
                    return ins
                pg.op("pe", inproj_u, reads=[c.hTB] + wiB, writes=[psB[3]])
                pg.op("act", lambda e: e.activation(out=uT[:, 0:512], in_=ps[3][:], func=AF.Gelu), reads=[psB[3]], writes=[c.uTB])

            def Bband(c):
                pcur, pprev, dTb = c.pcur, c.pprev, c.dTb
                if c.is_s:
                    def band(e):
                        for g in range(4):
                            o = ps[6][:, g * 128:(g + 1) * 128]
                            e.matmul(o, lhsT=pcur[:, g * 128:(g + 1) * 128], rhs=bands[:, 12 + g, :], start=True, stop=False)
                            e.matmul(o, lhsT=spast[:, 0, g * 128:(g + 1) * 128], rhs=bands[:, 16 + g, :], start=False, stop=False)
                            ins = e.matmul(o, lhsT=spast[:, 1, g * 128:(g + 1) * 128], rhs=bands[:, 20 + g, :], start=False, stop=True)
                        return ins
                    rd = [c.pcurB, spastB, bandsB]
                elif c.ptile == 0:
                    def band(e):
                        for g in range(4):
                            ins = e.matmul(ps[6][:, g * 128:(g + 1) * 128], lhsT=pcur[:, g * 128:(g + 1) * 128],
                                           rhs=bands[:, g, :], start=True, stop=True)
                        return ins
                    rd = [c.pcurB, bandsB]
                else:
                    def band(e):
                        for g in range(4):
                            o = ps[6][:, g * 128:(g + 1) * 128]
                            e.matmul(o, lhsT=pcur[:, g * 128:(g + 1) * 128], rhs=bands[:, 4 + g, :], start=True, stop=False)
                            ins = e.matmul(o, lhsT=pprev[:, g * 128:(g + 1) * 128], rhs=bands[:, 8 + g, :], start=False, stop=True)
                        return ins
                    rd = [c.pcurB, c.pprevB, bandsB]
                pg.op("pe", band, reads=rd, writes=[psB[6]])
                pg.op("act", lambda e: e.activation(out=dTb, in_=ps[6][:], func=AF.Copy), reads=[psB[6]], writes=[c.dTbB])

            def Bpoolw(c):
                dTb, mixT = c.dTb, c.mixT

                def poolw(e):
                    for g in range(4):
                        ins = e.matmul(ps[6][:, g * 128:(g + 1) * 128], lhsT=pw[:, g, :], rhs=dTb[:, g * 128:(g + 1) * 128],
                                       start=True, stop=True)
                    return ins
                pg.op("pe", poolw, reads=[c.dTbB, pwB], writes=[psB[6]])
                first = (not c.is_s) and c.ptile == 0
                for g in range(4):
                    sc_ap = psc[:, l * 4 + g:l * 4 + g + 1]
                    if first:
                        pg.op("dve", lambda e, g=g, sc_ap=sc_ap: e.scalar_tensor_tensor(
                            out=mixT[:, g, :], in0=ps[6][:, g * 128:(g + 1) * 128], scalar=sc_ap, in1=rc1[:, g, :],
                            op0=ALU.mult, op1=ALU.mult), reads=[psB[6], parB, rc1B], writes=[c.mixTB])
                    else:
                        pg.op("dve", lambda e, g=g, sc_ap=sc_ap: e.tensor_scalar(
                            out=mixT[:, g, :], in0=ps[6][:, g * 128:(g + 1) * 128], scalar1=sc_ap, scalar2=1.0 / WINS[g],
                            op0=ALU.mult, op1=ALU.mult), reads=[psB[6], parB], writes=[c.mixTB])

            def Bspat(c):
                vnb, tmp, uT, mixF = c.vnb, c.tmp, c.uT, c.mixF
                wsm, wsmB = (wsS, wsSB) if c.is_s else (wsP, wsPB)
                bst, bstB = (bsS, bsSB) if c.is_s else (bsP, bsPB)

                def spat(e):
                    for hh_ in range(4):
                        ins = e.matmul(ps[3][:, hh_ * 128:(hh_ + 1) * 128], lhsT=vnb[:, hh_ * 128:(hh_ + 1) * 128],
                                       rhs=wsm[:, hh_, :], start=True, stop=True)
                    return ins
                pg.op("pe", spat, reads=[c.vnbB, wsmB], writes=[psB[3]])
                pg.op("dve", lambda e: e.tensor_tensor(out=tmp[:, 0:512], in0=ps[3][:], in1=bst[:], op=ALU.add),
                      reads=[psB[3], bstB], writes=[c.tmpB])
                pg.op("dve", lambda e: e.tensor_tensor(out=mixF[:, 512:1024], in0=tmp[:, 0:512], in1=uT[:, 0:512], op=ALU.mult),
                      reads=[c.tmpB, c.uTB], writes=[c.mixTB])

            def Cout(c):
                mixT, slot = c.mixT, c.slot

                def outproj(e):
                    for k in range(KD):
                        e.matmul(ps[4][:], lhsT=mixT[:, k, :], rhs=wo[:, k, 0:512], start=(k == 0), stop=(k == KD - 1))
                        ins = e.matmul(ps[5][:], lhsT=mixT[:, k, :], rhs=wo[:, k, 512:1024], start=(k == 0), stop=(k == KD - 1))
                    return ins
                pg.op("pe", outproj, reads=[c.mixTB] + woB, writes=[psB[4], psB[5]])
                pg.op("dve", lambda e: e.tensor_tensor(out=xs[:, slot, 0:512], in0=ps[4][:], in1=xs[:, slot, 0:512], op=ALU.add),
                      reads=[psB[4], xB[slot]], writes=[xB[slot]])
                pg.op("dve", lambda e: e.tensor_tensor(out=xs[:, slot, 512:1024], in0=ps[5][:], in1=xs[:, slot, 512:1024], op=ALU.add),
                      reads=[psB[5], xB[slot]], writes=[xB[slot]])

            def ok(i):
                return 0 <= i < n
            for r in range(-2, n + 1):
                t, t1, t2 = r, r + 1, r + 2
                pump(1, 0)
                if ok(t):
                    Bband(C[t])
                if ok(t2):
                    N1a(C[t2])
                if ok(t):
                    Bspat(C[t])
                if ok(t - 1):
                    N2a(C[t - 1])
                if ok(t1):
                    A2pv(C[t1])
                if ok(t):
                    Bpoolw(C[t])
                if ok(t1):
                    A3pv(C[t1])
                if ok(t1):
                    A2u(C[t1])
                if ok(t - 1):
                    N2b(C[t - 1])
                if ok(t):
                    Cout(C[t])
                if ok(t2):
                    N1b(C[t2])

        ffn_state = {"n": 0, "gs": 0, "ae": 0, "hh": 0, "y": 0, "fifo": [], "cs": 0, "wdq": []}

        def ffn_phase(half, l, subtiles, mid_hook=None):
            for j, (c0, gn) in enumerate(GROUPS):
                if j == 5 and mid_hook is not None:
                    mid_hook()
                n = ffn_state["n"]
                s = n % 2
                wg_, wu_, fB = wgs[s], wus[s], fsB[s]
                flush_urgent(n)
                wd_, fdB_ = wds[n % 3], fdB[n % 3]
                last_of_group = None
                for si, (skind, slots) in enumerate(subtiles):
                    is_s = skind == "s"
                    ntok = 128 * len(slots)
                    col0 = slots[0] * 128
                    hi = 1 + ffn_state["hh"] % 3
                    ffn_state["hh"] += 1
                    hh = H[hi][:].rearrange("p (g n) -> p g n", g=G)
                    hhB = HB[hi]
                    want_cs = is_s or (half == 1 and si == len(subtiles) - 1)
                    if want_cs:
                        csi = ffn_state["cs"] % 2
                        ffn_state["cs"] += 1
                    if is_s:
                        pg.dma("sp", f"scs{csi}", [(scs[csi][:, 0:gn * 128], sc_d[l][:, c0 * 128:(c0 + gn) * 128])],
                               writes=[scsB[csi]])

                    gsl = []
                    fifo = ffn_state["fifo"]
                    for ci in range(gn):
                        c = c0 + ci
                        ga, gb = (1, 2) if (ffn_state["gs"] % 2 == 0) else (3, 0)
                        if is_s:
                            ga = gb = 1 if (ffn_state["gs"] % 2 == 0) else 3
                        ffn_state["gs"] += 1
                        uo = 128 if is_s else 0

                        def gu(e, ci=ci, ga=ga, gb=gb, ntok=ntok, col0=col0, wg_=wg_, wu_=wu_, uo=uo, is_s=is_s,
                               csi=(csi if want_cs else 0)):
                            if is_s:
                                e.transpose(ps[ga][:, 256:288], scs[csi][:, ci * 128:(ci + 1) * 128], identf[0:32, 0:32])
                            for k in range(KD):
                                e.matmul(ps[ga][:, 0:ntok], lhsT=wg_[:, k, ci * 128:(ci + 1) * 128],
                                         rhs=h2T[:, k, col0:col0 + ntok], start=(k == 0), stop=(k == KD - 1))
                            for k in range(KD):
                                ins = e.matmul(ps[gb][:, uo:uo + ntok], lhsT=wu_[:, k, ci * 128:(ci + 1) * 128],
                                               rhs=h2T[:, k, col0:col0 + ntok], start=(k == 0), stop=(k == KD - 1))
                            return ins
                        pg.op("pe", gu, reads=fB + [h2B[t] for t in slots] + ([scsB[csi], identfB] if is_s else []),
                              writes=[psB[ga], psB[gb]])
                        if len(fifo) > 1:
                            fifo.pop(0)()
                        pump(1, 1)
                        par = ffn_state["ae"] % 2
                        ffn_state["ae"] += 1
                        a1, a1B = S[2 * par], SBf[2 * par]
                        a0, a0B = S[2 * par + 1], SBf[2 * par + 1]
                        acc, accB, ge, geB = S[4 + par], SBf[4 + par], S[6 + par], SBf[6 + par]
                        w0 = cw[:, (l * 3 + 0) * NCH + c:(l * 3 + 0) * NCH + c + 1]
                        w1 = cw[:, (l * 3 + 1) * NCH + c:(l * 3 + 1) * NCH + c + 1]
                        w2 = cw[:, (l * 3 + 2) * NCH + c:(l * 3 + 2) * NCH + c + 1]
                        bb = cb[:, l * NCH + c:l * NCH + c + 1]
                        if is_s:
                            a13 = a1[:, 0:160].rearrange("p (q j) -> p q j", j=10)
                            a03 = a0[:, 0:160].rearrange("p (q j) -> p q j", j=10)
                            a3 = acc[:, 0:128].rearrange("p (q j) -> p q j", j=8)
                            gp3 = ps[ga][:, 0:128].rearrange("p (q j) -> p q j", j=8)
                            sc3 = ps[ga][:, 256:288].rearrange("p (q r) -> p q r", r=2)
                            pg.op("act", lambda e, a3=a3, gp3=gp3, w2=w2, bb=bb: e.activation(
                                out=a3, in_=gp3, func=AF.Identity, scale=w2, bias=bb), reads=[psB[ga], parB], writes=[accB])
                            pg.op("act", lambda e, a13=a13, gp3=gp3, w1=w1: e.activation(
                                out=a13[:, :, 2:10], in_=gp3, func=AF.Copy, scale=w1), reads=[psB[ga], parB], writes=[a1B])
                            pg.op("act", lambda e, a13=a13, sc3=sc3, w1=w1: e.activation(
                                out=a13[:, :, 0:2], in_=sc3, func=AF.Copy, scale=w1), reads=[psB[ga], parB], writes=[a1B])
                            pg.op("act", lambda e, a03=a03, gp3=gp3, w0=w0: e.activation(
                                out=a03[:, :, 2:10], in_=gp3, func=AF.Copy, scale=w0), reads=[psB[ga], parB], writes=[a0B])
                            pg.op("act", lambda e, a03=a03, sc3=sc3, w0=w0: e.activation(
                                out=a03[:, :, 0:2], in_=sc3, func=AF.Copy, scale=w0), reads=[psB[ga], parB], writes=[a0B])
                            pg.op("act", lambda e, gp3=gp3, ci=ci, csi=csi: e.activation(
                                out=csl[csi][:, ci, :].rearrange("p (q r) -> p q r", r=2), in_=gp3[:, :, 6:8], func=AF.Copy),
                                reads=[psB[ga]], writes=[cslB[csi]])
                            pg.op("pool", lambda e, a3=a3, a13=a13: e.tensor_tensor(out=a3, in0=a3, in1=a13[:, :, 1:9], op=ALU.add),
                                  reads=[accB, a1B], writes=[accB])
                            pg.op("pool", lambda e, a3=a3, a03=a03: e.tensor_tensor(out=a3, in0=a3, in1=a03[:, :, 0:8], op=ALU.add),
                                  reads=[accB, a0B], writes=[accB])
                            pg.op("pe", lambda e, ci=ci, csi=csi, ga=ga: e.transpose(
                                ps[ga][0:32, 320:448], csl[csi][:, ci, :], identf[:]), reads=[cslB[csi], identfB], writes=[psB[ga]])
                            pg.op("act", lambda e, ci=ci, csi=csi, ga=ga: e.activation(
                                out=cso[csi][0:32, ci * 128:(ci + 1) * 128], in_=ps[ga][0:32, 320:448], func=AF.Copy),
                                reads=[psB[ga]], writes=[csoB[csi]])
                        else:
                            last_p = si == len([x for x in subtiles if x[0] == "p"]) - 1
                            if si == 0:
                                if half == 0:
                                    pg.op("act", lambda e, a1=a1: e.activation(out=a1[:, 0:2], in_=cst_t[:, 2:4], func=AF.Copy),
                                          reads=[cstB], writes=[a1B])
                                    pg.op("act", lambda e, a0=a0: e.activation(out=a0[:, 0:2], in_=cst_t[:, 2:4], func=AF.Copy),
                                          reads=[cstB], writes=[a0B])
                                else:
                                    pg.op("act", lambda e, a1=a1, l=l, c=c: e.activation(out=a1[:, 1:2], in_=gcar[:, l, c, 0:1], func=AF.Copy),
                                          reads=[gcarB[l]], writes=[a1B])
                                    pg.op("act", lambda e, a0=a0, l=l, c=c: e.activation(out=a0[:, 0:2], in_=gcar[:, l, c, 1:3], func=AF.Copy),
                                          reads=[gcarB[l]], writes=[a0B])
                            pg.op("act", lambda e, acc=acc, ga=ga, w2=w2, bb=bb: e.activation(
                                out=acc[:, 0:512], in_=ps[ga][:], func=AF.Identity, scale=w2, bias=bb),
                                reads=[psB[ga], parB], writes=[accB])
                            pg.op("act", lambda e, a1=a1, ga=ga, w1=w1: e.activation(
                                out=a1[:, 2:514], in_=ps[ga][:], func=AF.Copy, scale=w1), reads=[psB[ga], parB], writes=[a1B])
                            pg.op("act", lambda e, a0=a0, ga=ga, w0=w0: e.activation(
                                out=a0[:, 2:514], in_=ps[ga][:], func=AF.Copy, scale=w0), reads=[psB[ga], parB], writes=[a0B])
                            if want_cs:
                                pg.op("act", lambda e, ci=ci, csi=csi, ga=ga: e.activation(
                                    out=csl[csi][:, ci, 0:2], in_=ps[ga][:, 510:512], func=AF.Copy), reads=[psB[ga]], writes=[cslB[csi]])
                                pg.dma("sp", f"o_cp{csi}_{ci}", [(ncp_o[l][:, c * 128:(c + 1) * 128].rearrange("r p -> p r"), csl[csi][:, ci, 0:2])],
                                       reads=[cslB[csi]], is_output=True, slow=True)
                            pg.op("pool", lambda e, acc=acc, a1=a1: e.tensor_tensor(
                                out=acc[:, 0:512], in0=acc[:, 0:512], in1=a1[:, 1:513], op=ALU.add), reads=[accB, a1B], writes=[accB])
                            pg.op("pool", lambda e, acc=acc, a0=a0: e.tensor_tensor(
                                out=acc[:, 0:512], in0=acc[:, 0:512], in1=a0[:, 0:512], op=ALU.add), reads=[accB, a0B], writes=[accB])
                            if half == 0 and last_p:
                                pg.op("act", lambda e, a1=a1, l=l, c=c: e.activation(out=gcar[:, l, c, 0:1], in_=a1[:, 513:514], func=AF.Copy),
                                      reads=[a1B], writes=[gcarB[l]])
                                pg.op("act", lambda e, a0=a0, l=l, c=c: e.activation(out=gcar[:, l, c, 1:3], in_=a0[:, 512:514], func=AF.Copy),
                                      reads=[a0B], writes=[gcarB[l]])
                            elif not last_p:
                                pg.op("act", lambda e, a1=a1: e.activation(out=a1[:, 1:2], in_=a1[:, 513:514], func=AF.Copy),
                                      reads=[a1B], writes=[a1B])
                                pg.op("act", lambda e, a0=a0: e.activation(out=a0[:, 0:2], in_=a0[:, 512:514], func=AF.Copy),
                                      reads=[a0B], writes=[a0B])
                        pg.op("act", lambda e, acc=acc, ge=ge, ntok=ntok: e.activation(out=ge[:, 0:ntok], in_=acc[:, 0:ntok], func=AF.Gelu),
                              reads=[accB], writes=[geB])
                        pg.op("dve", lambda e, ge=ge, hh=hh, ci=ci, gb=gb, ntok=ntok, uo=uo: e.tensor_tensor(
                            out=hh[:, ci, 0:ntok], in0=ge[:, 0:ntok], in1=ps[gb][:, uo:uo + ntok], op=ALU.mult),
                            reads=[geB, psB[gb]], writes=[hhB])

                    if want_cs and is_s:
                        pg.dma("sp", f"o_cs{csi}", [(ncs_o[l][:, c0 * 128:(c0 + gn) * 128], cso[csi][0:32, 0:gn * 128])],
                               reads=[csoB[csi]], is_output=True)

                    def down(sel, slots=slots, hh=hh, hhB=hhB, gn=gn, wd_=wd_, fB=fdB_):
                        for li in sel:
                            slot = slots[li]
                            ya, yb = (4, 5) if ffn_state["y"] % 2 == 0 else (6, 7)
                            ffn_state["y"] += 1

                            def dn(e, li=li, ya=ya, yb=yb):
                                for ci in range(gn):
                                    e.matmul(ps[ya][:], lhsT=hh[:, ci, li * 128:(li + 1) * 128], rhs=wd_[:, ci, 0:512],
                                             start=(ci == 0), stop=(ci == gn - 1))
                                    ins = e.matmul(ps[yb][:], lhsT=hh[:, ci, li * 128:(li + 1) * 128], rhs=wd_[:, ci, 512:1024],
                                                   start=(ci == 0), stop=(ci == gn - 1))
                                return ins
                            pg.op("pe", dn, reads=[hhB, fB], writes=[psB[ya], psB[yb]])
                            pg.op("dve", lambda e, slot=slot, ya=ya: e.tensor_tensor(
                                out=xs[:, slot, 0:512], in0=ps[ya][:], in1=xs[:, slot, 0:512], op=ALU.add),
                                reads=[psB[ya], xB[slot]], writes=[xB[slot]])
                            pg.op("dve", lambda e, slot=slot, yb=yb: e.tensor_tensor(
                                out=xs[:, slot, 512:1024], in0=ps[yb][:], in1=xs[:, slot, 512:1024], op=ALU.add),
                                reads=[psB[yb], xB[slot]], writes=[xB[slot]])
                    nt = len(slots)
                    hook_n = n if si == len(subtiles) - 1 else None
                    pieces = [list(range(0, (nt + 1) // 2)), list(range((nt + 1) // 2, nt))]
                    pieces = [p_ for p_ in pieces if p_]
                    for pi, p_ in enumerate(pieces):
                        def piece(p_=p_, down=down, last=(pi == len(pieces) - 1), hook_n=hook_n):
                            down(p_)
                            if last and hook_n is not None:
                                if ffn_state["wdq"]:
                                    emit_wd_load(ffn_state["wdq"].pop(0))
                                ffn_state["wdq"].append(hook_n + 3)
                        fifo.append(piece)
                emit_ffn_load(n + 2)
                ffn_state["n"] += 1

        def flush_down():
            fifo = ffn_state["fifo"]
            while fifo:
                fifo.pop(0)()

        fin_state = {"i": 0}

        def final_tile(slot, row0):
            rs, rsB = rms_stats(xs[:, slot, :], xB[slot], D)
            items = []
            bufs = []
            for hf in range(2):
                i = fin_state["i"] % 4
                fin_state["i"] += 1
                pg.op("dve", lambda e, i=i, hf=hf: e.scalar_tensor_tensor(
                    out=S[i][:, 0:512], in0=xs[:, slot, hf * 512:(hf + 1) * 512], scalar=rs, in1=g1bc[:, hf * 512:(hf + 1) * 512],
                    op0=ALU.mult, op1=ALU.mult), reads=[xB[slot], rsB, g1B], writes=[SBf[i]])
                items.append((y_all[row0:row0 + 128, hf * 512:(hf + 1) * 512], S[i][:, 0:512]))
                bufs.append(SBf[i])
            pg.dma("sp", f"o_y{(fin_state['i'] // 2) % 2}", items, reads=bufs, is_output=True)

        halves = [
            ([(i, "p", i) for i in range(8)] + [(8, "s", None)], [("p", [0, 1, 2, 3]), ("p", [4, 5, 6, 7]), ("s", [8])]),
            ([(i, "p", 8 + i) for i in range(8)], [("p", [0, 1, 2, 3]), ("p", [4, 5, 6, 7])]),
        ]
        emit_mixer_load(0)
        emit_ffn_load(0)
        emit_ffn_load(1)
        emit_wd_load(0)
        emit_wd_load(1)
        emit_wd_load(2)
        flush_urgent(10 ** 9)
        flush_bg()
        def load_x(slot, kind, ptile):
            row0 = 2048 if kind == "s" else ptile * 128
            pg.dma("sp", f"x{slot}", [(xs[:, slot, :], x_all[row0:row0 + 128, :])], writes=[xB[slot]])

        for (slot, kind, ptile) in halves[0][0]:
            load_x(slot, kind, ptile)
        emit_layer_tables(0, 0)
        emit_ws_mask(0)
        for half, (tiles, subtiles) in enumerate(halves):
            for l in range(L):
                mixer_phase(half, l, tiles)
                hook = None
                if l + 1 < L:
                    emit_mixer_load(l + 1)
                    emit_layer_tables(l + 1, half)
                    hook = (lambda half=half: emit_ws_mask(half))
                else:
                    pg.dma("sp", "tab", [(g1bc[:], gf_d[0:1, :].partition_broadcast(128))], writes=[g1B])
                    if half == 0:
                        emit_mixer_load(0)
                        emit_layer_tables(0, 1, with_g1=False)
                        hook = (lambda: emit_ws_mask(1))
                ffn_phase(half, l, subtiles, hook)
                flush_down()
                flush_bg()
            for i, (slot, kind, ptile) in enumerate(tiles):
                row0 = 2048 if kind == "s" else ptile * 128
                final_tile(slot, row0)
                if half == 0 and i < len(halves[1][0]):
                    load_x(*halves[1][0][i])
            if half == 0:
                pg.dma("sp", "tab", [(g1bc[:], g1_d[0:1, :].partition_broadcast(128))], writes=[g1B])
        pg.finalize()
    return nc


def _consts():
    t = np.arange(128)
    s = np.arange(128)
    S_, T_ = np.meshgrid(s, t, indexing="ij")
    bands = np.zeros((24, 128, 128), np.float32)
    for g, w in enumerate(WINS):
        d = T_ - S_
        cur = ((d >= 0) & (d < w)).astype(np.float32)
        first = cur.copy()
        gen = cur.copy()
        cnt = np.minimum(t + 1, w).astype(np.float32)
        first[t, t] = 1.0 - cnt
        gen[t, t] = 1.0 - w
        bands[g] = first
        bands[4 + g] = gen
        dp = T_ + 128 - S_
        bands[8 + g] = ((dp >= 0) & (dp < w)).astype(np.float32)
        same = (S_ // 8) == (T_ // 8)
        sc = (same & (d >= 0) & (d < w)).astype(np.float32)
        sc[t, t] = 1.0 - w
        bands[12 + g] = sc
        for a in range(2):
            m = np.zeros((128, 128), np.float32)
            for r in range(120):
                q = a * 8 + r // 15
                j = r % 15
                for tt in range(128):
                    if tt // 8 == q and (tt % 8 + 15 - j) < w:
                        m[r, tt] = 1.0
            bands[16 + 4 * a + g] = m
    bands_h = np.ascontiguousarray(bands.transpose(1, 0, 2))
    maskP = (S_ <= T_).astype(np.float32)
    maskS = (((S_ // 8) == (T_ // 8)) & ((S_ % 8) <= (T_ % 8))).astype(np.float32)
    rc1 = np.stack([1.0 / np.minimum(t + 1, w) for w in WINS]).astype(np.float32).reshape(1, 512)
    return bands_h, maskP, maskS, rc1


_NC_CACHE = {}


def kernel(x_prompt, x_sample, state_pool, state_conv, norm1_g, w_in, pool_w, pool_scale,
           v_norm_g, w_spatial, b_spatial, w_out, norm2_g, w_gate, w_up, conv_w, conv_b,
           w_down, final_norm_g):
    f = lambda a: np.ascontiguousarray(np.asarray(a, dtype=np.float32))
    x_prompt, x_sample, state_pool, state_conv = f(x_prompt), f(x_sample), f(state_pool), f(state_conv)
    w_spatial, b_spatial = f(w_spatial), f(b_spatial)
    bands_h, maskP, maskS, rc1 = _consts()
    wsT = f(w_spatial.transpose(0, 3, 1, 2).reshape(L, 128, 512))
    wsS = f(np.tile(w_spatial[:, :, :8, :8].transpose(0, 3, 1, 2), (1, 16, 1, 16)).reshape(L, 128, 512))
    bsP = f(b_spatial.reshape(L, 512))
    bsS = f(np.tile(b_spatial[:, :, :8], (1, 1, 16)).reshape(L, 512))
    pscT = f(np.asarray(pool_scale, np.float32).reshape(L, 4, 128).transpose(2, 0, 1).reshape(128, L * 4))
    cwT = f(np.asarray(conv_w, np.float32).reshape(L, 3, NCH, 128).transpose(3, 0, 1, 2).reshape(128, L * 3 * NCH))
    cbT = f(np.asarray(conv_b, np.float32).reshape(L, NCH, 128).transpose(2, 0, 1).reshape(128, L * NCH))
    assert NCH % G == 0
    wg_r = f(w_gate).reshape(L, KD, 128, NG, G * 128).transpose(0, 3, 2, 1, 4).reshape(L, NG, 128, G * 1024)
    wu_r = f(w_up).reshape(L, KD, 128, NG, G * 128).transpose(0, 3, 2, 1, 4).reshape(L, NG, 128, G * 1024)
    wd_r = f(w_down).reshape(L, NG, G, 128, D).transpose(0, 1, 3, 2, 4).reshape(L, NG, 128, G * 1024)
    wgu_h = np.ascontiguousarray(np.concatenate([wg_r, wu_r], axis=3))
    wd_h = np.ascontiguousarray(wd_r)
    wi_h = f(f(w_in).reshape(L, KD, 128, WIN).transpose(0, 2, 1, 3).reshape(L, 128, KD * WIN))
    wo_h = f(f(w_out).reshape(L, KD, 128, D).transpose(0, 2, 1, 3).reshape(L, 128, KD * D))
    pw_h = f(f(pool_w).transpose(0, 2, 1, 3).reshape(L, 128, 512))
    shared = {
        "wi_h": wi_h, "wo_h": wo_h, "wgu_h": wgu_h, "wd_h": wd_h, "pw_h": pw_h, "wsT": wsT, "wsS": wsS, "g1": f(norm1_g), "g2": f(norm2_g),
        "gf": f(final_norm_g).reshape(1, D), "vg": f(v_norm_g), "bsP": bsP, "bsS": bsS,
        "pscT": pscT, "cwT": cwT, "cbT": cbT, "identf": np.eye(128, dtype=np.float32),
        "maskP": maskP, "maskS": maskS, "bands": bands_h, "rc1": rc1,
    }
    in_maps = []
    for c in range(8):
        m = dict(shared)
        m["x_all"] = f(np.concatenate([x_prompt[c], x_sample[16 * c:16 * c + 16].reshape(128, D)], axis=0))
        m["sp"] = f(state_pool[:, 16 * c:16 * c + 16].reshape(L, 240, 512))
        m["sc"] = f(state_conv[:, 16 * c:16 * c + 16].reshape(L, 32, FF))
        in_maps.append(m)
    if "nc" not in _NC_CACHE:
        _NC_CACHE["nc"] = build_program()
    res = run_bass_kernel_spmd(_NC_CACHE["nc"], in_maps, core_ids=list(range(8)))
    R = res.results
    y_prompt = np.stack([R[c]["y_all"][:2048] for c in range(8)]).astype(np.float32)
    y_sample = np.concatenate([R[c]["y_all"][2048:].reshape(16, 8, D) for c in range(8)], axis=0).astype(np.float32)
    npp = np.stack([R[c]["npp"] for c in range(8)], axis=1).astype(np.float32)
    nps = np.concatenate([R[c]["nps"] for c in range(8)], axis=1).astype(np.float32)
    ncp = np.stack([R[c]["ncp"] for c in range(8)], axis=1).astype(np.float32)
    ncs = np.concatenate([R[c]["ncs"].reshape(L, 16, 2, FF) for c in range(8)], axis=1).astype(np.float32)
    nvs = np.concatenate([R[c]["nvs"].reshape(L, 16, 8, 512) for c in range(8)], axis=1).astype(np.float32)
    return (y_prompt, y_sample, npp, nps, ncp, ncs, nvs)
```

```python
import numpy as np
from contextlib import ExitStack
import concourse.bass as bass
import concourse.mybir as mybir
from concourse.bass_utils import run_bass_kernel_spmd

F32 = mybir.dt.float32
BF16 = mybir.dt.bfloat16
AF = mybir.ActivationFunctionType
ALU = mybir.AluOpType

L = 4
D = 1024
KD = 8
WIN = 1536
FF = 2816
NCH = 22
G = 2
GROUPS = [(c0, min(G, NCH - c0)) for c0 in range(0, NCH, G)]
NG = len(GROUPS)
WINS = (2, 4, 8, 16)
EPS = 1e-6
NSLOT = 9
DBG = {"halves": (0, 1), "layers": L, "mixer": True, "ffn": True, "final": True, "stage": 99, "ntiles": 99, "tables": True}
STRICT = True


class Buf:
    __slots__ = ("name", "w", "r", "psum")

    def __init__(self, name, psum=False):
        self.name = name
        self.w = None
        self.r = {}
        self.psum = psum


class Prog:
    ENG = ("pe", "act", "dve", "pool", "sp")

    def __init__(self, nc):
        self.nc = nc
        self.stream = {k: [] for k in self.ENG}
        self.semh = {}
        self.cnt = {}
        self.known = {k: {} for k in self.ENG}
        for k in ("pe", "act", "dve", "pool"):
            self._sem(k)
        self.out_events = []

    def _sem(self, key):
        if key not in self.semh:
            self.semh[key] = self.nc.alloc_semaphore("s_" + key)
            self.cnt[key] = 0
        return self.semh[key]

    def _deps(self, e, reads, writes):
        need = {}

        def add(key, val, src, same_ok):
            if src == e and same_ok:
                return
            if need.get(key, 0) < val:
                need[key] = val
        for b in reads:
            if b.w is not None:
                add(b.w[0], b.w[1], b.w[2], e == "pe")
            if b.psum:
                for key, (val, src) in b.r.items():
                    if src != e:
                        add(key, val, src, False)
        for b in writes:
            if b.w is not None:
                add(b.w[0], b.w[1], b.w[2], e == "pe" or not STRICT)
            for key, (val, src) in b.r.items():
                add(key, val, src, e == "pe" or not STRICT)
        out = []
        kn = self.known[e]
        for key, val in need.items():
            if kn.get(key, 0) < val:
                kn[key] = val
                out.append((key, val))
        return out

    def _record(self, ev, reads, writes):
        key, val, src = ev
        for b in reads:
            b.r[key] = (val, src)
        for b in writes:
            b.w = ev
            b.r = {}

    def op(self, e, fn, reads=(), writes=()):
        waits = self._deps(e, reads, writes)
        self.cnt[e] += 1
        ev = (e, self.cnt[e], e)
        self.stream[e].append((waits, fn, (e, 1)))
        self._record(ev, reads, writes)
        return ev

    def dma(self, q, semkey, items, reads=(), writes=(), is_output=False, slow=False):
        self._sem(semkey)
        waits = self._deps(q, reads, writes)
        self.cnt[semkey] += 16 * len(items)
        ev = (semkey, self.cnt[semkey], "dma")

        def fn(eng, items=items):
            if slow:
                return [eng.dma_start(out=o, in_=i, allow_slow_non_contiguous=True) for (o, i) in items]
            return [eng.dma_start(out=o, in_=i) for (o, i) in items]
        self.stream[q].append((waits, fn, (semkey, 16)))
        self._record(ev, reads, writes)
        if is_output:
            self.out_events.append(ev)
        return ev

    def finalize(self):
        fin = {}
        for key, val, _ in self.out_events:
            fin[key] = max(fin.get(key, 0), val)
        self.stream["sp"].append(([(k, v) for k, v in fin.items()], None, None))
        nc = self.nc
        with nc.Block() as block:
            decos = {"pe": block.tensor, "act": block.scalar, "dve": block.vector,
                     "pool": block.gpsimd, "sp": block.sync}
            for e in self.ENG:
                def body(eng, e=e):
                    for waits, fn, inc in self.stream[e]:
                        for key, val in waits:
                            eng.wait_ge(self.semh[key], val)
                        if fn is None:
                            continue
                        r = fn(eng)
                        if isinstance(r, (list, tuple)):
                            for ins in r:
                                ins.then_inc(self.semh[inc[0]], inc[1])
                        else:
                            r.then_inc(self.semh[inc[0]], inc[1])
                decos[e](body)


def build_program():
    nc = bass.Bass("TRN2", target_bir_lowering=False)

    def din(n, s):
        return nc.dram_tensor(n, list(s), F32, kind="ExternalInput").ap()

    def dout(n, s):
        return nc.dram_tensor(n, list(s), F32, kind="ExternalOutput").ap()

    x_all = din("x_all", (2176, D))
    sp_d = din("sp", (L, 240, 512))
    sc_d = din("sc", (L, 32, FF))
    w_in = din("wi_h", (L, 128, KD * WIN))
    w_out = din("wo_h", (L, 128, KD * D))
    wgu_d = din("wgu_h", (L, NG, 128, 2 * G * 1024))
    wd_d = din("wd_h", (L, NG, 128, G * 1024))
    pool_w = din("pw_h", (L, 128, 512))
    wsT_d = din("wsT", (L, 128, 512))
    wsS_d = din("wsS", (L, 128, 512))
    g1_d = din("g1", (L, D))
    g2_d = din("g2", (L, D))
    gf_d = din("gf", (1, D))
    vg_d = din("vg", (L, 512))
    bsP_d = din("bsP", (L, 512))
    bsS_d = din("bsS", (L, 512))
    psc_d = din("pscT", (128, L * 4))
    cw_d = din("cwT", (128, L * 3 * NCH))
    cb_d = din("cbT", (128, L * NCH))
    identf_d = din("identf", (128, 128))
    maskP_d = din("maskP", (128, 128))
    maskS_d = din("maskS", (128, 128))
    bands_d = din("bands", (128, 24, 128))
    rc1_d = din("rc1", (1, 512))

    y_all = dout("y_all", (2176, D))
    npp_o = dout("npp", (L, 15, 512))
    nps_o = dout("nps", (L, 16, 15, 512))
    ncp_o = dout("ncp", (L, 2, FF))
    ncs_o = dout("ncs", (L, 32, FF))
    nvs_o = dout("nvs", (L, 128, 512))

    st = ExitStack()
    with st:
        def SB(n, s, d):
            return st.enter_context(nc.sbuf_tensor("sb_" + n, list(s), d))

        pg = Prog(nc)
        xs = SB("xs", (128, NSLOT, D), F32)
        xB = [Buf(f"x{i}") for i in range(NSLOT)]
        h2T = SB("h2T", (128, KD, NSLOT * 128), BF16)
        h2B = [Buf(f"h2_{i}") for i in range(NSLOT)]
        wi = SB("wi", (128, KD, WIN), BF16); wiB = [Buf(f"wi{k}") for k in range(KD)]
        wo = SB("wo", (128, KD, D), BF16); woB = [Buf(f"wo{k}") for k in range(KD)]
        pw = SB("pw", (128, 4, 128), BF16); pwB = Buf("pw")
        wsP = SB("wsP", (128, 4, 128), BF16); wsPB = Buf("wsP")
        wsS = SB("wsS", (128, 4, 128), BF16); wsSB = Buf("wsS")
        wsraw = SB("wsraw", (128, 2, 512), F32); wsrawB = Buf("wsraw")
        wsl = [SB(f"wsl{i}", (128, 2 * G * 1024), BF16) for i in range(2)]
        wgs = [t[:, 0:G * 1024].rearrange("p (k n) -> p k n", k=KD) for t in wsl]
        wus = [t[:, G * 1024:2 * G * 1024].rearrange("p (k n) -> p k n", k=KD) for t in wsl]
        wdl = [SB(f"wdl{i}", (128, G * 1024), BF16) for i in range(3)]
        wds = [t[:].rearrange("p (g n) -> p g n", g=G) for t in wdl]
        fdB = [Buf(f"wdslot{i}") for i in range(3)]
        fsB = [[Buf(f"ffnslot{i}g"), Buf(f"ffnslot{i}u")] for i in range(2)]
        identb = SB("identb", (128, 128), BF16); identbB = Buf("identb")
        identf = SB("identf_s", (128, 128), F32); identfB = Buf("identf")
        maskP = SB("maskP_s", (128, 128), F32); maskS = SB("maskS_s", (128, 128), F32); maskB = Buf("masks")
        bands = SB("bands_s", (128, 24, 128), BF16); bandsB = Buf("bands")
        rc1 = SB("rc1_s", (128, 4, 128), F32); rc1B = Buf("rc1")
        g1bc = SB("g1bc", (128, D), F32); g1B = Buf("g1bc")
        g2bc = SB("g2bc", (128, D), F32); g2B = Buf("g2bc")
        vgbc = SB("vgbc", (128, 512), F32); vgB = Buf("vgbc")
        bsP = SB("bsP_s", (128, 512), F32); bsPB = Buf("bsP")
        bsS = SB("bsS_s", (128, 512), F32); bsSB = Buf("bsS")
        psc = SB("psc", (128, L * 4), F32)
        cw = SB("cw", (128, L * 3 * NCH), F32)
        cb = SB("cb", (128, L * NCH), F32)
        parB = Buf("params")
        stats = SB("stats", (128, 64), F32)
        statB = [Buf(f"stat{i}") for i in range(64)]
        cst_t = SB("cconst", (128, 4), F32); cstB = Buf("cconst")
        pbf = [SB(f"pbf{i}", (128, 512), BF16) for i in range(3)]
        pbfB = [Buf(f"pbf{i}") for i in range(3)]
        pcar = SB("pcar", (128, L, 512), BF16); pcarB = [Buf(f"pcar{i}") for i in range(L)]
        spast = SB("spast", (128, 2, 512), BF16); spastB = Buf("spast")
        gcar = SB("gcar", (128, L, NCH, 4), F32); gcarB = [Buf(f"gcar{i}") for i in range(L)]
        scs = [SB(f"scs{i}", (32, G * 128), F32) for i in range(2)]; scsB = [Buf(f"scs{i}") for i in range(2)]
        cso = [SB(f"cso{i}", (32, G * 128), F32) for i in range(2)]; csoB = [Buf(f"cso{i}") for i in range(2)]
        csl = [SB(f"csl{i}", (128, G, 32), F32) for i in range(2)]; cslB = [Buf(f"csl{i}") for i in range(2)]
        S = [SB(f"S{i}", (128, 514), F32) for i in range(8)]
        SBf = [Buf(f"S{i}") for i in range(8)]
        H = [SB(f"H{i}", (128, 1024), BF16) for i in range(9)]
        HB = [Buf(f"H{i}") for i in range(9)]
        JK = [H[0]] + [SB(f"JK{i}", (128, 1024), BF16) for i in range(2)]
        JKB = [HB[0]] + [Buf(f"JK{i}") for i in range(2)]
        jk_i = [0]
        H7B = [Buf("H7a"), Buf("H7b")]
        H8B = [Buf("H8a"), Buf("H8b")]
        psall = st.enter_context(nc.psum_tensor("psall", [128, 8 * 512], F32))
        ps = [psall[:, i * 512:(i + 1) * 512] for i in range(8)]
        psB = [Buf(f"ps{i}", psum=True) for i in range(8)]
        ps0b = ps[0][:].bitcast(BF16).rearrange("p (k n) -> p k n", k=8)

        stat_i = [0]

        def new_stat():
            i = stat_i[0] % 64
            stat_i[0] += 1
            return stats[:, i:i + 1], statB[i]

        pg.dma("sp", "cst", [(identf[:], identf_d[:, :]), (maskP[:], maskP_d[:, :]), (maskS[:], maskS_d[:, :]),
                             (rc1[:].rearrange("p g t -> p (g t)"), rc1_d[0:1, :].partition_broadcast(128)),
                             (psc[:], psc_d[:, :]), (cw[:], cw_d[:, :]), (cb[:], cb_d[:, :])],
               writes=[identfB, maskB, rc1B, parB])
        pg.dma("pool", "cstb", [(bands[:], bands_d[:, :, :])], writes=[bandsB])
        pg.op("dve", lambda e: e.tensor_copy(out=identb[:], in_=identf[:]), reads=[identfB], writes=[identbB])
        pg.op("dve", lambda e: e.memset(cst_t[:, 0:1], EPS), writes=[cstB])
        pg.op("dve", lambda e: e.memset(cst_t[:, 1:2], -0.5), writes=[cstB])
        pg.op("dve", lambda e: e.memset(cst_t[:, 2:4], 0.0), writes=[cstB])
        pg.op("dve", lambda e: e.memset(spast[:], 0.0), writes=[spastB])

        ffn_items = [(h, l, j) for h in range(2) for l in range(L) for j in range(NG)]

        dq_u, dq_b = [], []

        def pump(nu=1, nb=1):
            for _ in range(nu):
                if dq_u:
                    dq_u.pop(0)[1]()
            for _ in range(nb):
                if dq_b:
                    dq_b.pop(0)()

        def flush_urgent(upto):
            while dq_u and dq_u[0][0] <= upto:
                dq_u.pop(0)[1]()

        def flush_bg():
            while dq_b:
                dq_b.pop(0)()

        def emit_ffn_load(n):
            if n >= len(ffn_items):
                return
            h, l, j = ffn_items[n]
            s = n % 2
            for hf in range(2):
                dq_u.append((n, lambda s=s, l=l, j=j, hf=hf: pg.dma(
                    "pool", f"ffn{s}{hf}", [(wsl[s][:, hf * G * 1024:(hf + 1) * G * 1024], wgu_d[l, j][:, hf * G * 1024:(hf + 1) * G * 1024])],
                    writes=[fsB[s][hf]])))

        def emit_wd_load(n):
            if n >= len(ffn_items):
                return
            h, l, j = ffn_items[n]
            s = n % 3
            dq_u.append((n, lambda s=s, l=l, j=j: pg.dma("pool", f"ffd{s}", [(wdl[s][:], wd_d[l, j])], writes=[fdB[s]])))

        def emit_mixer_load(l):
            for k in range(KD):
                dq_b.append(lambda k=k, l=l: pg.dma("pool", f"wi{k}", [(wi[:, k, :], w_in[l][:, k * WIN:(k + 1) * WIN])], writes=[wiB[k]]))
            for k in range(KD):
                dq_b.append(lambda k=k, l=l: pg.dma("pool", f"wo{k}", [(wo[:, k, :], w_out[l][:, k * D:(k + 1) * D])], writes=[woB[k]]))
            dq_b.append(lambda l=l: pg.dma("pool", "pw", [(pw[:].rearrange("p g d -> p (g d)"), pool_w[l])], writes=[pwB]))

        def emit_layer_tables(l, half, with_g1=True):
            if with_g1:
                pg.dma("sp", "tab", [(g1bc[:], g1_d[l:l + 1, :].partition_broadcast(128))], writes=[g1B])
            pg.dma("sp", "tab2", [(g2bc[:], g2_d[l:l + 1, :].partition_broadcast(128)),
                                  (vgbc[:], vg_d[l:l + 1, :].partition_broadcast(128)),
                                  (bsP[:], bsP_d[l:l + 1, :].partition_broadcast(128)),
                                  (bsS[:], bsS_d[l:l + 1, :].partition_broadcast(128))],
                   writes=[g2B, vgB, bsPB, bsSB])
            pg.dma("sp", "wsr", [(wsraw[:, 0, :], wsT_d[l]), (wsraw[:, 1, :], wsS_d[l])], writes=[wsrawB])
            if half == 0:
                pg.dma("pool", "spast", [(spast[0:120, :, :], sp_d[l].rearrange("(a r) c -> r a c", a=2))],
                       writes=[spastB])
                pg.dma("sp", "o_nps_past", [(nps_o[l][:, 0:7, :], sp_d[l].rearrange("(q j) c -> q j c", j=15)[:, 8:15, :])],
                       is_output=True)

        def emit_ws_mask(half):
            for hh_ in range(4):
                pg.op("dve", lambda e, hh_=hh_: e.tensor_tensor(out=wsP[:, hh_, :], in0=wsraw[:, 0, hh_ * 128:(hh_ + 1) * 128],
                                                                in1=maskP[:], op=ALU.mult),
                      reads=[wsrawB, maskB], writes=[wsPB])
            if half == 0:
                for hh_ in range(4):
                    pg.op("dve", lambda e, hh_=hh_: e.tensor_tensor(out=wsS[:, hh_, :], in0=wsraw[:, 1, hh_ * 128:(hh_ + 1) * 128],
                                                                    in1=maskS[:], op=ALU.mult),
                          reads=[wsrawB, maskB], writes=[wsSB])

        ring = {"xn": 0, "hTm": 0, "mixT": 0, "pbf": 0}

        def rms_stats(x_ap, xbuf, width):
            ms, msB = new_stat()
            rs, rsB = new_stat()
            ji = jk_i[0] % 3
            jk_i[0] += 1
            pg.op("act", lambda e: e.activation(out=JK[ji][:, 0:width], in_=x_ap, func=AF.Square,
                                                scale=float(width) ** -0.5, accum_out=ms),
                  reads=[xbuf], writes=[JKB[ji], msB])
            pg.op("pool", lambda e: e.tensor_tensor(out=rs, in0=ms, in1=cst_t[:, 0:1], op=ALU.add),
                  reads=[msB, cstB], writes=[rsB])
            pg.op("pool", lambda e: e.tensor_tensor(out=rs, in0=rs, in1=cst_t[:, 1:2], op=ALU.pow),
                  reads=[rsB, cstB], writes=[rsB])
            return rs, rsB

        def norm_to_T(slot, gtab, gB, out_ap, outB):
            x_ap = xs[:, slot, :]
            rs, rsB = rms_stats(x_ap, xB[slot], D)
            i = 1 + ring["xn"] % 2
            ring["xn"] += 1
            xn, xnB = H[i], HB[i]
            pg.op("dve", lambda e: e.scalar_tensor_tensor(out=xn[:], in0=x_ap, scalar=rs, in1=gtab[:],
                                                          op0=ALU.mult, op1=ALU.mult),
                  reads=[xB[slot], rsB, gB], writes=[xnB])

            def tr(e):
                for k in range(KD):
                    ins = e.transpose(ps0b[:, k, :], xn[:, k * 128:(k + 1) * 128], identb[:])
                return ins
            pg.op("pe", tr, reads=[xnB, identbB], writes=[psB[0]])
            pg.op("act", lambda e: e.activation(out=out_ap, in_=ps0b, func=AF.Copy), reads=[psB[0]], writes=[outB])


        ps7b = ps[7][:].bitcast(BF16).rearrange("p (k n) -> p k n", k=KD)
        mstate = {"pbf": 0}

        class TC:
            pass

        def mixer_phase(half, l, tiles):
            n = len(tiles)
            C = []
            for idx, (slot, kind, ptile) in enumerate(tiles):
                c = TC()
                c.slot, c.kind, c.ptile, c.is_s, c.par = slot, kind, ptile, kind == "s", idx % 2
                c.hT = H[3 + c.par][:].rearrange("p (k n) -> p k n", k=KD)
                c.hTB = HB[3 + c.par]
                if (not c.is_s) and ptile == 7:
                    c.pcur, c.pcurB = pcar[:, l, :], pcarB[l]
                else:
                    r = mstate["pbf"] % 3
                    mstate["pbf"] += 1
                    c.pcur, c.pcurB = pbf[r][:], pbfB[r]
                if c.is_s or ptile == 0:
                    c.pprev, c.pprevB = None, None
                elif ptile == 8:
                    c.pprev, c.pprevB = pcar[:, l, :], pcarB[l]
                else:
                    c.pprev, c.pprevB = C[idx - 1].pcur, C[idx - 1].pcurB
                c.gv, c.gvB = S[c.par], SBf[c.par]
                c.tmp, c.tmpB = S[2 + c.par], SBf[2 + c.par]
                c.uT, c.uTB = S[4 + c.par], SBf[4 + c.par]
                c.vnb, c.vnbB = H[7][:, c.par * 512:(c.par + 1) * 512], H7B[c.par]
                c.dTb, c.dTbB = H[8][:, c.par * 512:(c.par + 1) * 512], H8B[c.par]
                c.mixF, c.mixT, c.mixTB = H[5 + c.par], H[5 + c.par][:].rearrange("p (k n) -> p k n", k=KD), HB[5 + c.par]
                C.append(c)

            def norm_a(c, gtab, gB, xn, xnB):
                x_ap = xs[:, c.slot, :]
                rs, rsB = rms_stats(x_ap, xB[c.slot], D)
                pg.op("dve", lambda e: e.scalar_tensor_tensor(out=xn[:], in0=x_ap, scalar=rs, in1=gtab[:],
                                                              op0=ALU.mult, op1=ALU.mult),
                      reads=[xB[c.slot], rsB, gB], writes=[xnB])

            def norm_b(xn, xnB, pst, pstB, out_ap, outB):
                def tr(e):
                    for k in range(KD):
                        ins = e.transpose(pst[:, k, :], xn[:, k * 128:(k + 1) * 128], identb[:])
                    return ins
                pg.op("pe", tr, reads=[xnB, identbB], writes=[pstB])
                pg.op("act", lambda e: e.activation(out=out_ap, in_=pst, func=AF.Copy), reads=[pstB], writes=[outB])

            def N1a(c):
                norm_a(c, g1bc, g1B, H[1], HB[1])

            def N1b(c):
                norm_b(H[1], HB[1], ps0b, psB[0], c.hT, c.hTB)

            def N2a(c):
                norm_a(c, g2bc, g2B, H[2], HB[2])

            def N2b(c):
                norm_b(H[2], HB[2], ps7b, psB[7], h2T[:, :, c.slot * 128:(c.slot + 1) * 128], h2B[c.slot])

            def A2pv(c):
                hT = c.hT

                def inproj(e):
                    for k in range(KD):
                        e.matmul(ps[1][:], lhsT=hT[:, k, :], rhs=wi[:, k, 0:512], start=(k == 0), stop=(k == KD - 1))
                        ins = e.matmul(ps[2][:], lhsT=hT[:, k, :], rhs=wi[:, k, 1024:1536], start=(k == 0), stop=(k == KD - 1))
                    return ins
                pg.op("pe", inproj, reads=[c.hTB] + wiB, writes=[psB[1], psB[2]])

            def A3pv(c):
                pcur, gv, vnb = c.pcur, c.gv, c.vnb
                pg.op("act", lambda e: e.activation(out=pcur, in_=ps[1][:], func=AF.Copy), reads=[psB[1]], writes=[c.pcurB])
                if c.is_s or c.ptile == 15:
                    pg.op("act", lambda e: e.activation(out=S[6][:, 0:512], in_=ps[1][:], func=AF.Copy),
                          reads=[psB[1]], writes=[SBf[6]])
                    if c.is_s:
                        pg.dma("sp", "o_nps", [(nps_o[l][q, 7:15, :], S[6][q * 8:(q + 1) * 8, 0:512]) for q in range(16)],
                               reads=[SBf[6]], is_output=True)
                    else:
                        pg.dma("sp", "o_npp", [(npp_o[l], S[6][113:128, 0:512])], reads=[SBf[6]], is_output=True)
                pg.op("act", lambda e: e.activation(out=gv[:, 0:512], in_=ps[2][:], func=AF.Gelu), reads=[psB[2]], writes=[c.gvB])
                rs, rsB = rms_stats(gv[:, 0:512], c.gvB, 512)
                if c.is_s:
                    pg.op("dve", lambda e: e.scalar_tensor_tensor(out=S[7][:, 0:512], in0=gv[:, 0:512], scalar=rs, in1=vgbc[:],
                                                                  op0=ALU.mult, op1=ALU.mult),
                          reads=[c.gvB, rsB, vgB], writes=[SBf[7]])
                    pg.op("dve", lambda e: e.tensor_copy(out=vnb, in_=S[7][:, 0:512]), reads=[SBf[7]], writes=[c.vnbB])
                    pg.dma("sp", "o_nvs", [(nvs_o[l], S[7][:, 0:512])], reads=[SBf[7]], is_output=True)
                else:
                    pg.op("dve", lambda e: e.scalar_tensor_tensor(out=vnb, in0=gv[:, 0:512], scalar=rs, in1=vgbc[:],
                                                                  op0=ALU.mult, op1=ALU.mult),
                          reads=[c.gvB, rsB, vgB], writes=[c.vnbB])

            def A2u(c):
                hT, uT = c.hT, c.uT

                def inproj_u(e):
                    for j in range(4):
                        for k in range(KD):
                            ins = e.matmul(ps[3][:, j * 128:(j + 1) * 128], lhsT=wi[:, k, 512 + j * 128:512 + (j + 1) * 128],
                                           rhs=hT[:, k, :], start=(k == 0), stop=(k == KD - 1))
                    return ins
                pg.op("pe", inproj_u, reads=[c.hTB] + wiB, writes=[psB[3]])
                pg.op("act", lambda e: e.activation(out=uT[:, 0:512], in_=ps[3][:], func=AF.Gelu), reads=[psB[3]], writes=[c.uTB])

            def Bband(c):
                pcur, pprev, dTb = c.pcur, c.pprev, c.dTb
                if c.is_s:
                    def band(e):
                        for g in range(4):
                            o = ps[6][:, g * 128:(g + 1) * 128]
                            e.matmul(o, lhsT=pcur[:, g * 128:(g + 1) * 128], rhs=bands[:, 12 + g, :], start=True, stop=False)
                            e.matmul(o, lhsT=spast[:, 0, g * 128:(g + 1) * 128], rhs=bands[:, 16 + g, :], start=False, stop=False)
                            ins = e.matmul(o, lhsT=spast[:, 1, g * 128:(g + 1) * 128], rhs=bands[:, 20 + g, :], start=False, stop=True)
                        return ins
                    rd = [c.pcurB, spastB, bandsB]
                elif c.ptile == 0:
                    def band(e):
                        for g in range(4):
                            ins = e.matmul(ps[6][:, g * 128:(g + 1) * 128], lhsT=pcur[:, g * 128:(g + 1) * 128],
                                           rhs=bands[:, g, :], start=True, stop=True)
                        return ins
                    rd = [c.pcurB, bandsB]
                else:
                    def band(e):
                        for g in range(4):
                            o = ps[6][:, g * 128:(g + 1) * 128]
                            e.matmul(o, lhsT=pcur[:, g * 128:(g + 1) * 128], rhs=bands[:, 4 + g, :], start=True, stop=False)
                            ins = e.matmul(o, lhsT=pprev[:, g * 128:(g + 1) * 128], rhs=bands[:, 8 + g, :], start=False, stop=True)
                        return ins
                    rd = [c.pcurB, c.pprevB, bandsB]
                pg.op("pe", band, reads=rd, writes=[psB[6]])
                pg.op("act", lambda e: e.activation(out=dTb, in_=ps[6][:], func=AF.Copy), reads=[psB[6]], writes=[c.dTbB])

            def Bpoolw(c):
                dTb, mixT = c.dTb, c.mixT

                def poolw(e):
                    for g in range(4):
                        ins = e.matmul(ps[6][:, g * 128:(g + 1) * 128], lhsT=pw[:, g, :], rhs=dTb[:, g * 128:(g + 1) * 128],
                                       start=True, stop=True)
                    return ins
                pg.op("pe", poolw, reads=[c.dTbB, pwB], writes=[psB[6]])
                first = (not c.is_s) and c.ptile == 0
                for g in range(4):
                    sc_ap = psc[:, l * 4 + g:l * 4 + g + 1]
                    if first:
                        pg.op("dve", lambda e, g=g, sc_ap=sc_ap: e.scalar_tensor_tensor(
                            out=mixT[:, g, :], in0=ps[6][:, g * 128:(g + 1) * 128], scalar=sc_ap, in1=rc1[:, g, :],
                            op0=ALU.mult, op1=ALU.mult), reads=[psB[6], parB, rc1B], writes=[c.mixTB])
                    else:
                        pg.op("dve", lambda e, g=g, sc_ap=sc_ap: e.tensor_scalar(
                            out=mixT[:, g, :], in0=ps[6][:, g * 128:(g + 1) * 128], scalar1=sc_ap, scalar2=1.0 / WINS[g],
                            op0=ALU.mult, op1=ALU.mult), reads=[psB[6], parB], writes=[c.mixTB])

            def Bspat(c):
                vnb, tmp, uT, mixF = c.vnb, c.tmp, c.uT, c.mixF
                wsm, wsmB = (wsS, wsSB) if c.is_s else (wsP, wsPB)
                bst, bstB = (bsS, bsSB) if c.is_s else (bsP, bsPB)

                def spat(e):
                    for hh_ in range(4):
                        ins = e.matmul(ps[3][:, hh_ * 128:(hh_ + 1) * 128], lhsT=vnb[:, hh_ * 128:(hh_ + 1) * 128],
                                       rhs=wsm[:, hh_, :], start=True, stop=True)
                    return ins
                pg.op("pe", spat, reads=[c.vnbB, wsmB], writes=[psB[3]])
                pg.op("dve", lambda e: e.tensor_tensor(out=tmp[:, 0:512], in0=ps[3][:], in1=bst[:], op=ALU.add),
                      reads=[psB[3], bstB], writes=[c.tmpB])
                pg.op("dve", lambda e: e.tensor_tensor(out=mixF[:, 512:1024], in0=tmp[:, 0:512], in1=uT[:, 0:512], op=ALU.mult),
                      reads=[c.tmpB, c.uTB], writes=[c.mixTB])

            def Cout(c):
                mixT, slot = c.mixT, c.slot

                def outproj(e):
                    for k in range(KD):
                        e.matmul(ps[4][:], lhsT=mixT[:, k, :], rhs=wo[:, k, 0:512], start=(k == 0), stop=(k == KD - 1))
                        ins = e.matmul(ps[5][:], lhsT=mixT[:, k, :], rhs=wo[:, k, 512:1024], start=(k == 0), stop=(k == KD - 1))
                    return ins
                pg.op("pe", outproj, reads=[c.mixTB] + woB, writes=[psB[4], psB[5]])
                pg.op("dve", lambda e: e.tensor_tensor(out=xs[:, slot, :], in0=psall[:, 4 * 512:6 * 512], in1=xs[:, slot, :], op=ALU.add),
                      reads=[psB[4], psB[5], xB[slot]], writes=[xB[slot]])

            def ok(i):
                return 0 <= i < n
            for r in range(-2, n + 1):
                t, t1, t2 = r, r + 1, r + 2
                pump(1, 0)
                if ok(t):
                    Bband(C[t])
                if ok(t2):
                    N1a(C[t2])
                if ok(t):
                    Bspat(C[t])
                if ok(t - 1):
                    N2a(C[t - 1])
                if ok(t1):
                    A2pv(C[t1])
                if ok(t):
                    Bpoolw(C[t])
                if ok(t1):
                    A3pv(C[t1])
                if ok(t1):
                    A2u(C[t1])
                if ok(t - 1):
                    N2b(C[t - 1])
                if ok(t):
                    Cout(C[t])
                if ok(t2):
                    N1b(C[t2])

        ffn_state = {"n": 0, "gs": 0, "ae": 0, "hh": 0, "y": 0, "fifo": [], "cs": 0, "wdq": []}

        def ffn_phase(half, l, subtiles, mid_hook=None):
            for j, (c0, gn) in enumerate(GROUPS):
                if j == 5 and mid_hook is not None:
                    mid_hook()
                n = ffn_state["n"]
                s = n % 2
                wg_, wu_, fB = wgs[s], wus[s], fsB[s]
                flush_urgent(n)
                wd_, fdB_ = wds[n % 3], fdB[n % 3]
                last_of_group = None
                for si, (skind, slots) in enumerate(subtiles):
                    is_s = skind == "s"
                    ntok = 128 * len(slots)
                    col0 = slots[0] * 128
                    hi = 1 + ffn_state["hh"] % 3
                    ffn_state["hh"] += 1
                    hh = H[hi][:].rearrange("p (g n) -> p g n", g=G)
                    hhB = HB[hi]
                    want_cs = is_s or (half == 1 and si == len(subtiles) - 1)
                    if want_cs:
                        csi = ffn_state["cs"] % 2
                        ffn_state["cs"] += 1
                    if is_s:
                        pg.dma("sp", f"scs{csi}", [(scs[csi][:, 0:gn * 128], sc_d[l][:, c0 * 128:(c0 + gn) * 128])],
                               writes=[scsB[csi]])

                    gsl = []
                    fifo = ffn_state["fifo"]
                    for ci in range(gn):
                        c = c0 + ci
                        ga, gb = (1, 2) if (ffn_state["gs"] % 2 == 0) else (3, 0)
                        if is_s:
                            ga = gb = 1 if (ffn_state["gs"] % 2 == 0) else 3
                        ffn_state["gs"] += 1
                        uo = 128 if is_s else 0

                        def gu(e, ci=ci, ga=ga, gb=gb, ntok=ntok, col0=col0, wg_=wg_, wu_=wu_, uo=uo, is_s=is_s,
                               csi=(csi if want_cs else 0)):
                            if is_s:
                                e.transpose(ps[ga][:, 256:288], scs[csi][:, ci * 128:(ci + 1) * 128], identf[0:32, 0:32])
                            for k in range(KD):
                                e.matmul(ps[ga][:, 0:ntok], lhsT=wg_[:, k, ci * 128:(ci + 1) * 128],
                                         rhs=h2T[:, k, col0:col0 + ntok], start=(k == 0), stop=(k == KD - 1))
                            for k in range(KD):
                                ins = e.matmul(ps[gb][:, uo:uo + ntok], lhsT=wu_[:, k, ci * 128:(ci + 1) * 128],
                                               rhs=h2T[:, k, col0:col0 + ntok], start=(k == 0), stop=(k == KD - 1))
                            return ins
                        pg.op("pe", gu, reads=fB + [h2B[t] for t in slots] + ([scsB[csi], identfB] if is_s else []),
                              writes=[psB[ga], psB[gb]])
                        if len(fifo) > 1:
                            fifo.pop(0)()
                        pump(1, 1)
                        key = ("gsr", ci)
                        r = ffn_state.get(key, 0)
                        ffn_state[key] = r + 1
                        gst, gstB = S[2 * ci + r % 2], SBf[2 * ci + r % 2]
                        gprev, gprevB = S[2 * ci + (r + 1) % 2], SBf[2 * ci + (r + 1) % 2]
                        ai = 4 + ffn_state["ae"] % 2
                        ei = 6 + ffn_state["ae"] % 2
                        ffn_state["ae"] += 1
                        acc, accB, ge, geB = S[ai], SBf[ai], S[ei], SBf[ei]
                        w0 = cw[:, (l * 3 + 0) * NCH + c:(l * 3 + 0) * NCH + c + 1]
                        w1 = cw[:, (l * 3 + 1) * NCH + c:(l * 3 + 1) * NCH + c + 1]
                        w2 = cw[:, (l * 3 + 2) * NCH + c:(l * 3 + 2) * NCH + c + 1]
                        bb = cb[:, l * NCH + c:l * NCH + c + 1]
                        if is_s:
                            g3 = gst[:, 0:160].rearrange("p (q j) -> p q j", j=10)
                            a3 = acc[:, 0:128].rearrange("p (q j) -> p q j", j=8)
                            gp3 = ps[ga][:, 0:128].rearrange("p (q j) -> p q j", j=8)
                            pg.op("act", lambda e, g3=g3, gp3=gp3: e.activation(out=g3[:, :, 2:10], in_=gp3, func=AF.Copy),
                                  reads=[psB[ga]], writes=[gstB])
                            pg.op("act", lambda e, g3=g3, ci=ci, ga=ga: e.activation(
                                out=g3[:, :, 0:2], in_=ps[ga][:, 256:288].rearrange("p (q r) -> p q r", r=2),
                                func=AF.Copy), reads=[psB[ga]], writes=[gstB])
                            pg.op("act", lambda e, a3=a3, gp3=gp3, w2=w2, bb=bb: e.activation(
                                out=a3, in_=gp3, func=AF.Identity, scale=w2, bias=bb), reads=[psB[ga], parB], writes=[accB])
                            pg.op("dve", lambda e, a3=a3, g3=g3, w1=w1: e.scalar_tensor_tensor(
                                out=a3, in0=g3[:, :, 1:9], scalar=w1, in1=a3, op0=ALU.mult, op1=ALU.add),
                                reads=[gstB, accB, parB], writes=[accB])
                            pg.op("dve", lambda e, a3=a3, g3=g3, w0=w0: e.scalar_tensor_tensor(
                                out=a3, in0=g3[:, :, 0:8], scalar=w0, in1=a3, op0=ALU.mult, op1=ALU.add),
                                reads=[gstB, accB, parB], writes=[accB])
                            pg.op("act", lambda e, g3=g3, ci=ci, csi=csi: e.activation(
                                out=csl[csi][:, ci, :].rearrange("p (q r) -> p q r", r=2), in_=g3[:, :, 8:10], func=AF.Copy),
                                reads=[gstB], writes=[cslB[csi]])
                            pg.op("pe", lambda e, ci=ci, csi=csi, ga=ga: e.transpose(
                                ps[ga][0:32, 320:448], csl[csi][:, ci, :], identf[:]), reads=[cslB[csi], identfB], writes=[psB[ga]])
                            pg.op("act", lambda e, ci=ci, csi=csi, ga=ga: e.activation(
                                out=cso[csi][0:32, ci * 128:(ci + 1) * 128], in_=ps[ga][0:32, 320:448], func=AF.Copy),
                                reads=[psB[ga]], writes=[csoB[csi]])
                        else:
                            pg.op("act", lambda e, gst=gst, ga=ga: e.activation(out=gst[:, 2:514], in_=ps[ga][:], func=AF.Copy),
                                  reads=[psB[ga]], writes=[gstB])
                            if si == 0:
                                if half == 0:
                                    hsrc, hsrcB = cst_t[:, 2:4], cstB
                                else:
                                    hsrc, hsrcB = gcar[:, l, c, 0:2], gcarB[l]
                            else:
                                hsrc, hsrcB = gprev[:, 512:514], gprevB
                            pg.op("act", lambda e, gst=gst, hsrc=hsrc: e.activation(out=gst[:, 0:2], in_=hsrc, func=AF.Copy),
                                  reads=[hsrcB], writes=[gstB])
                            pg.op("act", lambda e, acc=acc, ga=ga, w2=w2, bb=bb: e.activation(
                                out=acc[:, 0:512], in_=ps[ga][:], func=AF.Identity, scale=w2, bias=bb),
                                reads=[psB[ga], parB], writes=[accB])
                            pg.op("dve", lambda e, acc=acc, gst=gst, w1=w1: e.scalar_tensor_tensor(
                                out=acc[:, 0:512], in0=gst[:, 1:513], scalar=w1, in1=acc[:, 0:512], op0=ALU.mult, op1=ALU.add),
                                reads=[gstB, accB, parB], writes=[accB])
                            pg.op("dve", lambda e, acc=acc, gst=gst, w0=w0: e.scalar_tensor_tensor(
                                out=acc[:, 0:512], in0=gst[:, 0:512], scalar=w0, in1=acc[:, 0:512], op0=ALU.mult, op1=ALU.add),
                                reads=[gstB, accB, parB], writes=[accB])
                            if half == 0 and si == len([x for x in subtiles if x[0] == "p"]) - 1:
                                pg.op("act", lambda e, gst=gst, l=l, c=c: e.activation(out=gcar[:, l, c, 0:2], in_=gst[:, 512:514],
                                                                                      func=AF.Copy),
                                      reads=[gstB], writes=[gcarB[l]])
                            if want_cs:
                                pg.dma("sp", f"o_cp{ci}_{r % 2}", [(ncp_o[l][:, c * 128:(c + 1) * 128].rearrange("r p -> p r"), gst[:, 512:514])],
                                       reads=[gstB], is_output=True, slow=True)
                        pg.op("act", lambda e, acc=acc, ge=ge, ntok=ntok: e.activation(out=ge[:, 0:ntok], in_=acc[:, 0:ntok], func=AF.Gelu),
                              reads=[accB], writes=[geB])
                        pg.op("dve", lambda e, ge=ge, hh=hh, ci=ci, gb=gb, ntok=ntok, uo=uo: e.tensor_tensor(
                            out=hh[:, ci, 0:ntok], in0=ge[:, 0:ntok], in1=ps[gb][:, uo:uo + ntok], op=ALU.mult),
                            reads=[geB, psB[gb]], writes=[hhB])
                    if want_cs and is_s:
                        pg.dma("sp", f"o_cs{csi}", [(ncs_o[l][:, c0 * 128:(c0 + gn) * 128], cso[csi][0:32, 0:gn * 128])],
                               reads=[csoB[csi]], is_output=True)

                    def down(sel, slots=slots, hh=hh, hhB=hhB, gn=gn, wd_=wd_, fB=fdB_):
                        for li in sel:
                            slot = slots[li]
                            ya, yb = (4, 5) if ffn_state["y"] % 2 == 0 else (6, 7)
                            ffn_state["y"] += 1

                            def dn(e, li=li, ya=ya, yb=yb):
                                for ci in range(gn):
                                    e.matmul(ps[ya][:], lhsT=hh[:, ci, li * 128:(li + 1) * 128], rhs=wd_[:, ci, 0:512],
                                             start=(ci == 0), stop=(ci == gn - 1))
                                    ins = e.matmul(ps[yb][:], lhsT=hh[:, ci, li * 128:(li + 1) * 128], rhs=wd_[:, ci, 512:1024],
                                                   start=(ci == 0), stop=(ci == gn - 1))
                                return ins
                            pg.op("pe", dn, reads=[hhB, fB], writes=[psB[ya], psB[yb]])
                            pg.op("dve", lambda e, slot=slot, ya=ya: e.tensor_tensor(
                                out=xs[:, slot, :], in0=psall[:, ya * 512:(ya + 2) * 512], in1=xs[:, slot, :], op=ALU.add),
                                reads=[psB[ya], psB[yb], xB[slot]], writes=[xB[slot]])
                    nt = len(slots)
                    hook_n = n if si == len(subtiles) - 1 else None
                    pieces = [list(range(0, (nt + 1) // 2)), list(range((nt + 1) // 2, nt))]
                    pieces = [p_ for p_ in pieces if p_]
                    for pi, p_ in enumerate(pieces):
                        def piece(p_=p_, down=down, last=(pi == len(pieces) - 1), hook_n=hook_n):
                            down(p_)
                            if last and hook_n is not None:
                                if ffn_state["wdq"]:
                                    emit_wd_load(ffn_state["wdq"].pop(0))
                                ffn_state["wdq"].append(hook_n + 3)
                        fifo.append(piece)
                emit_ffn_load(n + 2)
                ffn_state["n"] += 1

        def flush_down():
            fifo = ffn_state["fifo"]
            while fifo:
                fifo.pop(0)()

        fin_state = {"i": 0}

        def final_tile(slot, row0):
            rs, rsB = rms_stats(xs[:, slot, :], xB[slot], D)
            items = []
            bufs = []
            for hf in range(2):
                i = fin_state["i"] % 4
                fin_state["i"] += 1
                pg.op("dve", lambda e, i=i, hf=hf: e.scalar_tensor_tensor(
                    out=S[i][:, 0:512], in0=xs[:, slot, hf * 512:(hf + 1) * 512], scalar=rs, in1=g1bc[:, hf * 512:(hf + 1) * 512],
                    op0=ALU.mult, op1=ALU.mult), reads=[xB[slot], rsB, g1B], writes=[SBf[i]])
                items.append((y_all[row0:row0 + 128, hf * 512:(hf + 1) * 512], S[i][:, 0:512]))
                bufs.append(SBf[i])
            pg.dma("sp", f"o_y{(fin_state['i'] // 2) % 2}", items, reads=bufs, is_output=True)

        halves = [
            ([(i, "p", i) for i in range(8)] + [(8, "s", None)], [("p", [0, 1, 2, 3]), ("p", [4, 5, 6, 7]), ("s", [8])]),
            ([(i, "p", 8 + i) for i in range(8)], [("p", [0, 1, 2, 3]), ("p", [4, 5, 6, 7])]),
        ]
        emit_mixer_load(0)
        emit_ffn_load(0)
        emit_ffn_load(1)
        emit_wd_load(0)
        emit_wd_load(1)
        emit_wd_load(2)
        flush_urgent(10 ** 9)
        flush_bg()
        def load_x(slot, kind, ptile):
            row0 = 2048 if kind == "s" else ptile * 128
            pg.dma("sp", f"x{slot}", [(xs[:, slot, :], x_all[row0:row0 + 128, :])], writes=[xB[slot]])

        for (slot, kind, ptile) in halves[0][0]:
            load_x(slot, kind, ptile)
        emit_layer_tables(0, 0)
        emit_ws_mask(0)
        for half, (tiles, subtiles) in enumerate(halves):
            for l in range(L):
                mixer_phase(half, l, tiles)
                hook = None
                if l + 1 < L:
                    emit_mixer_load(l + 1)
                    emit_layer_tables(l + 1, half)
                    hook = (lambda half=half: emit_ws_mask(half))
                else:
                    pg.dma("sp", "tab", [(g1bc[:], gf_d[0:1, :].partition_broadcast(128))], writes=[g1B])
                    if half == 0:
                        emit_mixer_load(0)
                        emit_layer_tables(0, 1, with_g1=False)
                        hook = (lambda: emit_ws_mask(1))
                ffn_phase(half, l, subtiles, hook)
                flush_down()
                flush_bg()
            for i, (slot, kind, ptile) in enumerate(tiles):
                row0 = 2048 if kind == "s" else ptile * 128
                final_tile(slot, row0)
                if half == 0 and i < len(halves[1][0]):
                    load_x(*halves[1][0][i])
            if half == 0:
                pg.dma("sp", "tab", [(g1bc[:], g1_d[0:1, :].partition_broadcast(128))], writes=[g1B])
        pg.finalize()
    return nc


def _consts():
    t = np.arange(128)
    s = np.arange(128)
    S_, T_ = np.meshgrid(s, t, indexing="ij")
    bands = np.zeros((24, 128, 128), np.float32)
    for g, w in enumerate(WINS):
        d = T_ - S_
        cur = ((d >= 0) & (d < w)).astype(np.float32)
        first = cur.copy()
        gen = cur.copy()
        cnt = np.minimum(t + 1, w).astype(np.float32)
        first[t, t] = 1.0 - cnt
        gen[t, t] = 1.0 - w
        bands[g] = first
        bands[4 + g] = gen
        dp = T_ + 128 - S_
        bands[8 + g] = ((dp >= 0) & (dp < w)).astype(np.float32)
        same = (S_ // 8) == (T_ // 8)
        sc = (same & (d >= 0) & (d < w)).astype(np.float32)
        sc[t, t] = 1.0 - w
        bands[12 + g] = sc
        for a in range(2):
            m = np.zeros((128, 128), np.float32)
            for r in range(120):
                q = a * 8 + r // 15
                j = r % 15
                for tt in range(128):
                    if tt // 8 == q and (tt % 8 + 15 - j) < w:
                        m[r, tt] = 1.0
            bands[16 + 4 * a + g] = m
    bands_h = np.ascontiguousarray(bands.transpose(1, 0, 2))
    maskP = (S_ <= T_).astype(np.float32)
    maskS = (((S_ // 8) == (T_ // 8)) & ((S_ % 8) <= (T_ % 8))).astype(np.float32)
    rc1 = np.stack([1.0 / np.minimum(t + 1, w) for w in WINS]).astype(np.float32).reshape(1, 512)
    return bands_h, maskP, maskS, rc1


_NC_CACHE = {}


def kernel(x_prompt, x_sample, state_pool, state_conv, norm1_g, w_in, pool_w, pool_scale,
           v_norm_g, w_spatial, b_spatial, w_out, norm2_g, w_gate, w_up, conv_w, conv_b,
           w_down, final_norm_g):
    f = lambda a: np.ascontiguousarray(np.asarray(a, dtype=np.float32))
    x_prompt, x_sample, state_pool, state_conv = f(x_prompt), f(x_sample), f(state_pool), f(state_conv)
    w_spatial, b_spatial = f(w_spatial), f(b_spatial)
    bands_h, maskP, maskS, rc1 = _consts()
    wsT = f(w_spatial.transpose(0, 3, 1, 2).reshape(L, 128, 512))
    wsS = f(np.tile(w_spatial[:, :, :8, :8].transpose(0, 3, 1, 2), (1, 16, 1, 16)).reshape(L, 128, 512))
    bsP = f(b_spatial.reshape(L, 512))
    bsS = f(np.tile(b_spatial[:, :, :8], (1, 1, 16)).reshape(L, 512))
    pscT = f(np.asarray(pool_scale, np.float32).reshape(L, 4, 128).transpose(2, 0, 1).reshape(128, L * 4))
    cwT = f(np.asarray(conv_w, np.float32).reshape(L, 3, NCH, 128).transpose(3, 0, 1, 2).reshape(128, L * 3 * NCH))
    cbT = f(np.asarray(conv_b, np.float32).reshape(L, NCH, 128).transpose(2, 0, 1).reshape(128, L * NCH))
    assert NCH % G == 0
    wg_r = f(w_gate).reshape(L, KD, 128, NG, G * 128).transpose(0, 3, 2, 1, 4).reshape(L, NG, 128, G * 1024)
    wu_r = f(w_up).reshape(L, KD, 128, NG, G * 128).transpose(0, 3, 2, 1, 4).reshape(L, NG, 128, G * 1024)
    wd_r = f(w_down).reshape(L, NG, G, 128, D).transpose(0, 1, 3, 2, 4).reshape(L, NG, 128, G * 1024)
    wgu_h = np.ascontiguousarray(np.concatenate([wg_r, wu_r], axis=3))
    wd_h = np.ascontiguousarray(wd_r)
    wi_h = f(f(w_in).reshape(L, KD, 128, WIN).transpose(0, 2, 1, 3).reshape(L, 128, KD * WIN))
    wo_h = f(f(w_out).reshape(L, KD, 128, D).transpose(0, 2, 1, 3).reshape(L, 128, KD * D))
    pw_h = f(f(pool_w).transpose(0, 2, 1, 3).reshape(L, 128, 512))
    shared = {
        "wi_h": wi_h, "wo_h": wo_h, "wgu_h": wgu_h, "wd_h": wd_h, "pw_h": pw_h, "wsT": wsT, "wsS": wsS, "g1": f(norm1_g), "g2": f(norm2_g),
        "gf": f(final_norm_g).reshape(1, D), "vg": f(v_norm_g), "bsP": bsP, "bsS": bsS,
        "pscT": pscT, "cwT": cwT, "cbT": cbT, "identf": np.eye(128, dtype=np.float32),
        "maskP": maskP, "maskS": maskS, "bands": bands_h, "rc1": rc1,
    }
    in_maps = []
    for c in range(8):
        m = dict(shared)
        m["x_all"] = f(np.concatenate([x_prompt[c], x_sample[16 * c:16 * c + 16].reshape(128, D)], axis=0))
        m["sp"] = f(state_pool[:, 16 * c:16 * c + 16].reshape(L, 240, 512))
        m["sc"] = f(state_conv[:, 16 * c:16 * c + 16].reshape(L, 32, FF))
        in_maps.append(m)
    if "nc" not in _NC_CACHE:
        _NC_CACHE["nc"] = build_program()
    res = run_bass_kernel_spmd(_NC_CACHE["nc"], in_maps, core_ids=list(range(8)))
    R = res.results
    y_prompt = np.stack([R[c]["y_all"][:2048] for c in range(8)]).astype(np.float32)
    y_sample = np.concatenate([R[c]["y_all"][2048:].reshape(16, 8, D) for c in range(8)], axis=0).astype(np.float32)
    npp = np.stack([R[c]["npp"] for c in range(8)], axis=1).astype(np.float32)
    nps = np.concatenate([R[c]["nps"] for c in range(8)], axis=1).astype(np.float32)
    ncp = np.stack([R[c]["ncp"] for c in range(8)], axis=1).astype(np.float32)
    ncs = np.concatenate([R[c]["ncs"].reshape(L, 16, 2, FF) for c in range(8)], axis=1).astype(np.float32)
    nvs = np.concatenate([R[c]["nvs"].reshape(L, 16, 8, 512) for c in range(8)], axis=1).astype(np.float32)
    return (y_prompt, y_sample, npp, nps, ncp, ncs, nvs)
```

```python
import numpy as np
from contextlib import ExitStack
import concourse.bass as bass
import concourse.mybir as mybir
from concourse.bass_utils import run_bass_kernel_spmd

F32 = mybir.dt.float32
BF16 = mybir.dt.bfloat16
AF = mybir.ActivationFunctionType
ALU = mybir.AluOpType

L = 4
D = 1024
KD = 8
WIN = 1536
FF = 2816
NCH = 22
G = 2
GROUPS = [(c0, min(G, NCH - c0)) for c0 in range(0, NCH, G)]
NG = len(GROUPS)
WINS = (2, 4, 8, 16)
EPS = 1e-6
NSLOT = 9
DBG = {"halves": (0, 1), "layers": L, "mixer": True, "ffn": True, "final": True, "stage": 99, "ntiles": 99, "tables": True}
STRICT = True


class Buf:
    __slots__ = ("name", "w", "r", "psum")

    def __init__(self, name, psum=False):
        self.name = name
        self.w = None
        self.r = {}
        self.psum = psum


class Prog:
    ENG = ("pe", "act", "dve", "pool", "sp")

    def __init__(self, nc):
        self.nc = nc
        self.stream = {k: [] for k in self.ENG}
        self.semh = {}
        self.cnt = {}
        self.known = {k: {} for k in self.ENG}
        for k in ("pe", "act", "dve", "pool"):
            self._sem(k)
        self.out_events = []

    def _sem(self, key):
        if key not in self.semh:
            self.semh[key] = self.nc.alloc_semaphore("s_" + key)
            self.cnt[key] = 0
        return self.semh[key]

    def _deps(self, e, reads, writes):
        need = {}

        def add(key, val, src, same_ok):
            if src == e and same_ok:
                return
            if need.get(key, 0) < val:
                need[key] = val
        for b in reads:
            if b.w is not None:
                add(b.w[0], b.w[1], b.w[2], e == "pe")
            if b.psum:
                for key, (val, src) in b.r.items():
                    if src != e:
                        add(key, val, src, False)
        for b in writes:
            if b.w is not None:
                add(b.w[0], b.w[1], b.w[2], e == "pe" or not STRICT)
            for key, (val, src) in b.r.items():
                add(key, val, src, e == "pe" or not STRICT)
        out = []
        kn = self.known[e]
        for key, val in need.items():
            if kn.get(key, 0) < val:
                kn[key] = val
                out.append((key, val))
        return out

    def _record(self, ev, reads, writes):
        key, val, src = ev
        for b in reads:
            b.r[key] = (val, src)
        for b in writes:
            b.w = ev
            b.r = {}

    def op(self, e, fn, reads=(), writes=()):
        waits = self._deps(e, reads, writes)
        self.cnt[e] += 1
        ev = (e, self.cnt[e], e)
        self.stream[e].append((waits, fn, (e, 1)))
        self._record(ev, reads, writes)
        return ev

    def dma(self, q, semkey, items, reads=(), writes=(), is_output=False, slow=False):
        self._sem(semkey)
        waits = self._deps(q, reads, writes)
        self.cnt[semkey] += 16 * len(items)
        ev = (semkey, self.cnt[semkey], "dma")

        def fn(eng, items=items):
            if slow:
                return [eng.dma_start(out=o, in_=i, allow_slow_non_contiguous=True) for (o, i) in items]
            return [eng.dma_start(out=o, in_=i) for (o, i) in items]
        self.stream[q].append((waits, fn, (semkey, 16)))
        self._record(ev, reads, writes)
        if is_output:
            self.out_events.append(ev)
        return ev

    def finalize(self):
        fin = {}
        for key, val, _ in self.out_events:
            fin[key] = max(fin.get(key, 0), val)
        self.stream["sp"].append(([(k, v) for k, v in fin.items()], None, None))
        nc = self.nc
        with nc.Block() as block:
            decos = {"pe": block.tensor, "act": block.scalar, "dve": block.vector,
                     "pool": block.gpsimd, "sp": block.sync}
            for e in self.ENG:
                def body(eng, e=e):
                    for waits, fn, inc in self.stream[e]:
                        for key, val in waits:
                            eng.wait_ge(self.semh[key], val)
                        if fn is None:
                            continue
                        r = fn(eng)
                        if isinstance(r, (list, tuple)):
                            for ins in r:
                                ins.then_inc(self.semh[inc[0]], inc[1])
                        else:
                            r.then_inc(self.semh[inc[0]], inc[1])
                decos[e](body)


def build_program():
    nc = bass.Bass("TRN2", target_bir_lowering=False)

    def din(n, s):
        return nc.dram_tensor(n, list(s), F32, kind="ExternalInput").ap()

    def dout(n, s):
        return nc.dram_tensor(n, list(s), F32, kind="ExternalOutput").ap()

    x_all = din("x_all", (2176, D))
    sp_d = din("sp", (L, 240, 512))
    sc_d = din("sc", (L, 32, FF))
    w_in = din("wi_h", (L, 128, KD * WIN))
    w_out = din("wo_h", (L, 128, KD * D))
    wgu_d = din("wgu_h", (L, NG, 128, 2 * G * 1024))
    wd_d = din("wd_h", (L, NG, 128, G * 1024))
    pool_w = din("pw_h", (L, 128, 512))
    wsT_d = din("wsT", (L, 128, 512))
    wsS_d = din("wsS", (L, 128, 512))
    g1_d = din("g1", (L, D))
    g2_d = din("g2", (L, D))
    gf_d = din("gf", (1, D))
    vg_d = din("vg", (L, 512))
    bsP_d = din("bsP", (L, 512))
    bsS_d = din("bsS", (L, 512))
    psc_d = din("pscT", (128, L * 4))
    cw_d = din("cwT", (128, L * 3 * NCH))
    cb_d = din("cbT", (128, L * NCH))
    identf_d = din("identf", (128, 128))
    maskP_d = din("maskP", (128, 128))
    maskS_d = din("maskS", (128, 128))
    bands_d = din("bands", (128, 24, 128))
    rc1_d = din("rc1", (1, 512))

    y_all = dout("y_all", (2176, D))
    npp_o = dout("npp", (L, 15, 512))
    nps_o = dout("nps", (L, 16, 15, 512))
    ncp_o = dout("ncp", (L, 2, FF))
    ncs_o = dout("ncs", (L, 32, FF))
    nvs_o = dout("nvs", (L, 128, 512))

    st = ExitStack()
    with st:
        def SB(n, s, d):
            return st.enter_context(nc.sbuf_tensor("sb_" + n, list(s), d))

        pg = Prog(nc)
        xs = SB("xs", (128, NSLOT, D), F32)
        xB = [Buf(f"x{i}") for i in range(NSLOT)]
        h2T = SB("h2T", (128, KD, NSLOT * 128), BF16)
        h2B = [Buf(f"h2_{i}") for i in range(NSLOT)]
        wi = SB("wi", (128, KD, WIN), BF16); wiB = [Buf(f"wi{k}") for k in range(KD)]
        wo = SB("wo", (128, KD, D), BF16); woB = [Buf(f"wo{k}") for k in range(KD)]
        pw = SB("pw", (128, 4, 128), BF16); pwB = Buf("pw")
        wsP = SB("wsP", (128, 4, 128), BF16); wsPB = Buf("wsP")
        wsS = SB("wsS", (128, 4, 128), BF16); wsSB = Buf("wsS")
        wsraw = SB("wsraw", (128, 2, 512), F32); wsrawB = Buf("wsraw")
        wsl = [SB(f"wsl{i}", (128, 2 * G * 1024), BF16) for i in range(2)]
        wgs = [t[:, 0:G * 1024].rearrange("p (k n) -> p k n", k=KD) for t in wsl]
        wus = [t[:, G * 1024:2 * G * 1024].rearrange("p (k n) -> p k n", k=KD) for t in wsl]
        wdl = [SB(f"wdl{i}", (128, G * 1024), BF16) for i in range(3)]
        wds = [t[:].rearrange("p (g n) -> p g n", g=G) for t in wdl]
        fdB = [Buf(f"wdslot{i}") for i in range(3)]
        fsB = [[Buf(f"ffnslot{i}g"), Buf(f"ffnslot{i}u")] for i in range(2)]
        identb = SB("identb", (128, 128), BF16); identbB = Buf("identb")
        identf = SB("identf_s", (128, 128), F32); identfB = Buf("identf")
        maskP = SB("maskP_s", (128, 128), F32); maskS = SB("maskS_s", (128, 128), F32); maskB = Buf("masks")
        bands = SB("bands_s", (128, 24, 128), BF16); bandsB = Buf("bands")
        rc1 = SB("rc1_s", (128, 4, 128), F32); rc1B = Buf("rc1")
        g1bc = SB("g1bc", (128, D), F32); g1B = Buf("g1bc")
        g2bc = SB("g2bc", (128, D), F32); g2B = Buf("g2bc")
        vgbc = SB("vgbc", (128, 512), F32); vgB = Buf("vgbc")
        bsP = SB("bsP_s", (128, 512), F32); bsPB = Buf("bsP")
        bsS = SB("bsS_s", (128, 512), F32); bsSB = Buf("bsS")
        psc = SB("psc", (128, L * 4), F32)
        cw = SB("cw", (128, L * 3 * NCH), F32)
        cb = SB("cb", (128, L * NCH), F32)
        parB = Buf("params")
        stats = SB("stats", (128, 64), F32)
        statB = [Buf(f"stat{i}") for i in range(64)]
        cst_t = SB("cconst", (128, 4), F32); cstB = Buf("cconst")
        pbf = [SB(f"pbf{i}", (128, 512), BF16) for i in range(3)]
        pbfB = [Buf(f"pbf{i}") for i in range(3)]
        pcar = SB("pcar", (128, L, 512), BF16); pcarB = [Buf(f"pcar{i}") for i in range(L)]
        spast = SB("spast", (128, 2, 512), BF16); spastB = Buf("spast")
        gcar = SB("gcar", (128, L, NCH, 4), F32); gcarB = [Buf(f"gcar{i}") for i in range(L)]
        scs = [SB(f"scs{i}", (32, G * 128), F32) for i in range(2)]; scsB = [Buf(f"scs{i}") for i in range(2)]
        cso = [SB(f"cso{i}", (32, G * 128), F32) for i in range(2)]; csoB = [Buf(f"cso{i}") for i in range(2)]
        csl = [SB(f"csl{i}", (128, G, 32), F32) for i in range(2)]; cslB = [Buf(f"csl{i}") for i in range(2)]
        S = [SB(f"S{i}", (128, 514), F32) for i in range(8)]
        SBf = [Buf(f"S{i}") for i in range(8)]
        H = [SB(f"H{i}", (128, 1024), BF16) for i in range(9)]
        HB = [Buf(f"H{i}") for i in range(9)]
        JK = [H[0]] + [SB(f"JK{i}", (128, 1024), BF16) for i in range(2)]
        JKB = [HB[0]] + [Buf(f"JK{i}") for i in range(2)]
        jk_i = [0]
        H7B = [Buf("H7a"), Buf("H7b")]
        H8B = [Buf("H8a"), Buf("H8b")]
        psall = st.enter_context(nc.psum_tensor("psall", [128, 8 * 512], F32))
        ps = [psall[:, i * 512:(i + 1) * 512] for i in range(8)]
        psB = [Buf(f"ps{i}", psum=True) for i in range(8)]
        ps0b = ps[0][:].bitcast(BF16).rearrange("p (k n) -> p k n", k=8)

        stat_i = [0]

        def new_stat():
            i = stat_i[0] % 64
            stat_i[0] += 1
            return stats[:, i:i + 1], statB[i]

        pg.dma("sp", "cst", [(identf[:], identf_d[:, :]), (maskP[:], maskP_d[:, :]), (maskS[:], maskS_d[:, :]),
                             (rc1[:].rearrange("p g t -> p (g t)"), rc1_d[0:1, :].partition_broadcast(128)),
                             (psc[:], psc_d[:, :]), (cw[:], cw_d[:, :]), (cb[:], cb_d[:, :])],
               writes=[identfB, maskB, rc1B, parB])
        pg.dma("pool", "cstb", [(bands[:], bands_d[:, :, :])], writes=[bandsB])
        pg.op("dve", lambda e: e.tensor_copy(out=identb[:], in_=identf[:]), reads=[identfB], writes=[identbB])
        pg.op("dve", lambda e: e.memset(cst_t[:, 0:1], EPS), writes=[cstB])
        pg.op("dve", lambda e: e.memset(cst_t[:, 1:2], -0.5), writes=[cstB])
        pg.op("dve", lambda e: e.memset(cst_t[:, 2:4], 0.0), writes=[cstB])
        pg.op("dve", lambda e: e.memset(spast[:], 0.0), writes=[spastB])

        ffn_items = [(h, l, j) for h in range(2) for l in range(L) for j in range(NG)]

        dq_u, dq_b = [], []

        def pump(nu=1, nb=1):
            for _ in range(nu):
                if dq_u:
                    dq_u.pop(0)[1]()
            for _ in range(nb):
                if dq_b:
                    dq_b.pop(0)()

        def flush_urgent(upto):
            while dq_u and dq_u[0][0] <= upto:
                dq_u.pop(0)[1]()

        def flush_bg():
            while dq_b:
                dq_b.pop(0)()

        def emit_ffn_load(n):
            if n >= len(ffn_items):
                return
            h, l, j = ffn_items[n]
            s = n % 2
            for hf in range(2):
                dq_u.append((n, lambda s=s, l=l, j=j, hf=hf: pg.dma(
                    "pool", f"ffn{s}{hf}", [(wsl[s][:, hf * G * 1024:(hf + 1) * G * 1024], wgu_d[l, j][:, hf * G * 1024:(hf + 1) * G * 1024])],
                    writes=[fsB[s][hf]])))

        def emit_wd_load(n):
            if n >= len(ffn_items):
                return
            h, l, j = ffn_items[n]
            s = n % 3
            dq_u.append((n, lambda s=s, l=l, j=j: pg.dma("pool", f"ffd{s}", [(wdl[s][:], wd_d[l, j])], writes=[fdB[s]])))

        def emit_mixer_load(l):
            for k in range(KD):
                dq_b.append(lambda k=k, l=l: pg.dma("pool", f"wi{k}", [(wi[:, k, :], w_in[l][:, k * WIN:(k + 1) * WIN])], writes=[wiB[k]]))
            for k in range(KD):
                dq_b.append(lambda k=k, l=l: pg.dma("pool", f"wo{k}", [(wo[:, k, :], w_out[l][:, k * D:(k + 1) * D])], writes=[woB[k]]))
            dq_b.append(lambda l=l: pg.dma("pool", "pw", [(pw[:].rearrange("p g d -> p (g d)"), pool_w[l])], writes=[pwB]))

        def emit_layer_tables(l, half, with_g1=True):
            if with_g1:
                pg.dma("sp", "tab", [(g1bc[:], g1_d[l:l + 1, :].partition_broadcast(128))], writes=[g1B])
            pg.dma("sp", "tab2", [(g2bc[:], g2_d[l:l + 1, :].partition_broadcast(128)),
                                  (vgbc[:], vg_d[l:l + 1, :].partition_broadcast(128)),
                                  (bsP[:], bsP_d[l:l + 1, :].partition_broadcast(128)),
                                  (bsS[:], bsS_d[l:l + 1, :].partition_broadcast(128))],
                   writes=[g2B, vgB, bsPB, bsSB])
            pg.dma("sp", "wsr", [(wsraw[:, 0, :], wsT_d[l]), (wsraw[:, 1, :], wsS_d[l])], writes=[wsrawB])
            if half == 0:
                pg.dma("pool", "spast", [(spast[0:120, :, :], sp_d[l].rearrange("(a r) c -> r a c", a=2))],
                       writes=[spastB])
                pg.dma("sp", "o_nps_past", [(nps_o[l][:, 0:7, :], sp_d[l].rearrange("(q j) c -> q j c", j=15)[:, 8:15, :])],
                       is_output=True)

        def emit_ws_mask(half):
            for hh_ in range(4):
                pg.op("dve", lambda e, hh_=hh_: e.tensor_tensor(out=wsP[:, hh_, :], in0=wsraw[:, 0, hh_ * 128:(hh_ + 1) * 128],
                                                                in1=maskP[:], op=ALU.mult),
                      reads=[wsrawB, maskB], writes=[wsPB])
            if half == 0:
                for hh_ in range(4):
                    pg.op("dve", lambda e, hh_=hh_: e.tensor_tensor(out=wsS[:, hh_, :], in0=wsraw[:, 1, hh_ * 128:(hh_ + 1) * 128],
                                                                    in1=maskS[:], op=ALU.mult),
                          reads=[wsrawB, maskB], writes=[wsSB])

        ring = {"xn": 0, "hTm": 0, "mixT": 0, "pbf": 0}

        def rms_stats(x_ap, xbuf, width):
            ms, msB = new_stat()
            rs, rsB = new_stat()
            ji = jk_i[0] % 3
            jk_i[0] += 1
            pg.op("act", lambda e: e.activation(out=JK[ji][:, 0:width], in_=x_ap, func=AF.Square,
                                                scale=float(width) ** -0.5, accum_out=ms),
                  reads=[xbuf], writes=[JKB[ji], msB])
            pg.op("pool", lambda e: e.tensor_tensor(out=rs, in0=ms, in1=cst_t[:, 0:1], op=ALU.add),
                  reads=[msB, cstB], writes=[rsB])
            pg.op("pool", lambda e: e.tensor_tensor(out=rs, in0=rs, in1=cst_t[:, 1:2], op=ALU.pow),
                  reads=[rsB, cstB], writes=[rsB])
            return rs, rsB

        def norm_to_T(slot, gtab, gB, out_ap, outB):
            x_ap = xs[:, slot, :]
            rs, rsB = rms_stats(x_ap, xB[slot], D)
            i = 1 + ring["xn"] % 2
            ring["xn"] += 1
            xn, xnB = H[i], HB[i]
            pg.op("dve", lambda e: e.scalar_tensor_tensor(out=xn[:], in0=x_ap, scalar=rs, in1=gtab[:],
                                                          op0=ALU.mult, op1=ALU.mult),
                  reads=[xB[slot], rsB, gB], writes=[xnB])

            def tr(e):
                for k in range(KD):
                    ins = e.transpose(ps0b[:, k, :], xn[:, k * 128:(k + 1) * 128], identb[:])
                return ins
            pg.op("pe", tr, reads=[xnB, identbB], writes=[psB[0]])
            pg.op("act", lambda e: e.activation(out=out_ap, in_=ps0b, func=AF.Copy), reads=[psB[0]], writes=[outB])


        ps7b = ps[7][:].bitcast(BF16).rearrange("p (k n) -> p k n", k=KD)
        mstate = {"pbf": 0}

        class TC:
            pass

        def mixer_phase(half, l, tiles):
            n = len(tiles)
            C = []
            for idx, (slot, kind, ptile) in enumerate(tiles):
                c = TC()
                c.slot, c.kind, c.ptile, c.is_s, c.par = slot, kind, ptile, kind == "s", idx % 2
                c.hT = H[3 + c.par][:].rearrange("p (k n) -> p k n", k=KD)
                c.hTB = HB[3 + c.par]
                if (not c.is_s) and ptile == 7:
                    c.pcur, c.pcurB = pcar[:, l, :], pcarB[l]
                else:
                    r = mstate["pbf"] % 3
                    mstate["pbf"] += 1
                    c.pcur, c.pcurB = pbf[r][:], pbfB[r]
                if c.is_s or ptile == 0:
                    c.pprev, c.pprevB = None, None
                elif ptile == 8:
                    c.pprev, c.pprevB = pcar[:, l, :], pcarB[l]
                else:
                    c.pprev, c.pprevB = C[idx - 1].pcur, C[idx - 1].pcurB
                c.gv, c.gvB = S[c.par], SBf[c.par]
                c.tmp, c.tmpB = S[2 + c.par], SBf[2 + c.par]
                c.uT, c.uTB = S[4 + c.par], SBf[4 + c.par]
                c.vnb, c.vnbB = H[7][:, c.par * 512:(c.par + 1) * 512], H7B[c.par]
                c.dTb, c.dTbB = H[8][:, c.par * 512:(c.par + 1) * 512], H8B[c.par]
                c.mixF, c.mixT, c.mixTB = H[5 + c.par], H[5 + c.par][:].rearrange("p (k n) -> p k n", k=KD), HB[5 + c.par]
                C.append(c)

            def norm_a(c, gtab, gB, xn, xnB):
                x_ap = xs[:, c.slot, :]
                rs, rsB = rms_stats(x_ap, xB[c.slot], D)
                pg.op("dve", lambda e: e.scalar_tensor_tensor(out=xn[:], in0=x_ap, scalar=rs, in1=gtab[:],
                                                              op0=ALU.mult, op1=ALU.mult),
                      reads=[xB[c.slot], rsB, gB], writes=[xnB])

            def norm_b(xn, xnB, pst, pstB, out_ap, outB):
                def tr(e):
                    for k in range(KD):
                        ins = e.transpose(pst[:, k, :], xn[:, k * 128:(k + 1) * 128], identb[:])
                    return ins
                pg.op("pe", tr, reads=[xnB, identbB], writes=[pstB])
                pg.op("act", lambda e: e.activation(out=out_ap, in_=pst, func=AF.Copy), reads=[pstB], writes=[outB])

            def N1a(c):
                norm_a(c, g1bc, g1B, H[1], HB[1])

            def N1b(c):
                norm_b(H[1], HB[1], ps0b, psB[0], c.hT, c.hTB)

            def N2a(c):
                norm_a(c, g2bc, g2B, H[2], HB[2])

            def N2b(c):
                norm_b(H[2], HB[2], ps7b, psB[7], h2T[:, :, c.slot * 128:(c.slot + 1) * 128], h2B[c.slot])

            def A2pv(c):
                hT = c.hT

                def inproj(e):
                    for k in range(KD):
                        e.matmul(ps[1][:], lhsT=hT[:, k, :], rhs=wi[:, k, 0:512], start=(k == 0), stop=(k == KD - 1))
                        ins = e.matmul(ps[2][:], lhsT=hT[:, k, :], rhs=wi[:, k, 1024:1536], start=(k == 0), stop=(k == KD - 1))
                    return ins
                pg.op("pe", inproj, reads=[c.hTB] + wiB, writes=[psB[1], psB[2]])

            def A3pv(c):
                pcur, gv, vnb = c.pcur, c.gv, c.vnb
                pg.op("act", lambda e: e.activation(out=pcur, in_=ps[1][:], func=AF.Copy), reads=[psB[1]], writes=[c.pcurB])
                if c.is_s or c.ptile == 15:
                    pg.op("act", lambda e: e.activation(out=S[6][:, 0:512], in_=ps[1][:], func=AF.Copy),
                          reads=[psB[1]], writes=[SBf[6]])
                    if c.is_s:
                        pg.dma("sp", "o_nps", [(nps_o[l][q, 7:15, :], S[6][q * 8:(q + 1) * 8, 0:512]) for q in range(16)],
                               reads=[SBf[6]], is_output=True)
                    else:
                        pg.dma("sp", "o_npp", [(npp_o[l], S[6][113:128, 0:512])], reads=[SBf[6]], is_output=True)
                pg.op("act", lambda e: e.activation(out=gv[:, 0:512], in_=ps[2][:], func=AF.Gelu), reads=[psB[2]], writes=[c.gvB])
                rs, rsB = rms_stats(gv[:, 0:512], c.gvB, 512)
                if c.is_s:
                    pg.op("dve", lambda e: e.scalar_tensor_tensor(out=S[7][:, 0:512], in0=gv[:, 0:512], scalar=rs, in1=vgbc[:],
                                                                  op0=ALU.mult, op1=ALU.mult),
                          reads=[c.gvB, rsB, vgB], writes=[SBf[7]])
                    pg.op("dve", lambda e: e.tensor_copy(out=vnb, in_=S[7][:, 0:512]), reads=[SBf[7]], writes=[c.vnbB])
                    pg.dma("sp", "o_nvs", [(nvs_o[l], S[7][:, 0:512])], reads=[SBf[7]], is_output=True)
                else:
                    pg.op("dve", lambda e: e.scalar_tensor_tensor(out=vnb, in0=gv[:, 0:512], scalar=rs, in1=vgbc[:],
                                                                  op0=ALU.mult, op1=ALU.mult),
                          reads=[c.gvB, rsB, vgB], writes=[c.vnbB])

            def A2u(c):
                hT, uT = c.hT, c.uT

                def inproj_u(e):
                    for j in range(4):
                        for k in range(KD):
                            ins = e.matmul(ps[3][:, j * 128:(j + 1) * 128], lhsT=wi[:, k, 512 + j * 128:512 + (j + 1) * 128],
                                           rhs=hT[:, k, :], start=(k == 0), stop=(k == KD - 1))
                    return ins
                pg.op("pe", inproj_u, reads=[c.hTB] + wiB, writes=[psB[3]])
                pg.op("act", lambda e: e.activation(out=uT[:, 0:512], in_=ps[3][:], func=AF.Gelu), reads=[psB[3]], writes=[c.uTB])

            def Bband(c):
                pcur, pprev, dTb = c.pcur, c.pprev, c.dTb
                if c.is_s:
                    def band(e):
                        for g in range(4):
                            o = ps[6][:, g * 128:(g + 1) * 128]
                            e.matmul(o, lhsT=pcur[:, g * 128:(g + 1) * 128], rhs=bands[:, 12 + g, :], start=True, stop=False)
                            e.matmul(o, lhsT=spast[:, 0, g * 128:(g + 1) * 128], rhs=bands[:, 16 + g, :], start=False, stop=False)
                            ins = e.matmul(o, lhsT=spast[:, 1, g * 128:(g + 1) * 128], rhs=bands[:, 20 + g, :], start=False, stop=True)
                        return ins
                    rd = [c.pcurB, spastB, bandsB]
                elif c.ptile == 0:
                    def band(e):
                        for g in range(4):
                            ins = e.matmul(ps[6][:, g * 128:(g + 1) * 128], lhsT=pcur[:, g * 128:(g + 1) * 128],
                                           rhs=bands[:, g, :], start=True, stop=True)
                        return ins
                    rd = [c.pcurB, bandsB]
                else:
                    def band(e):
                        for g in range(4):
                            o = ps[6][:, g * 128:(g + 1) * 128]
                            e.matmul(o, lhsT=pcur[:, g * 128:(g + 1) * 128], rhs=bands[:, 4 + g, :], start=True, stop=False)
                            ins = e.matmul(o, lhsT=pprev[:, g * 128:(g + 1) * 128], rhs=bands[:, 8 + g, :], start=False, stop=True)
                        return ins
                    rd = [c.pcurB, c.pprevB, bandsB]
                pg.op("pe", band, reads=rd, writes=[psB[6]])
                pg.op("act", lambda e: e.activation(out=dTb, in_=ps[6][:], func=AF.Copy), reads=[psB[6]], writes=[c.dTbB])

            def Bpoolw(c):
                dTb, mixT = c.dTb, c.mixT

                def poolw(e):
                    for g in range(4):
                        ins = e.matmul(ps[6][:, g * 128:(g + 1) * 128], lhsT=pw[:, g, :], rhs=dTb[:, g * 128:(g + 1) * 128],
                                       start=True, stop=True)
                    return ins
                pg.op("pe", poolw, reads=[c.dTbB, pwB], writes=[psB[6]])
                first = (not c.is_s) and c.ptile == 0
                for g in range(4):
                    sc_ap = psc[:, l * 4 + g:l * 4 + g + 1]
                    if first:
                        pg.op("dve", lambda e, g=g, sc_ap=sc_ap: e.scalar_tensor_tensor(
                            out=mixT[:, g, :], in0=ps[6][:, g * 128:(g + 1) * 128], scalar=sc_ap, in1=rc1[:, g, :],
                            op0=ALU.mult, op1=ALU.mult), reads=[psB[6], parB, rc1B], writes=[c.mixTB])
                    else:
                        pg.op("dve", lambda e, g=g, sc_ap=sc_ap: e.tensor_scalar(
                            out=mixT[:, g, :], in0=ps[6][:, g * 128:(g + 1) * 128], scalar1=sc_ap, scalar2=1.0 / WINS[g],
                            op0=ALU.mult, op1=ALU.mult), reads=[psB[6], parB], writes=[c.mixTB])

            def Bspat(c):
                vnb, tmp, uT, mixF = c.vnb, c.tmp, c.uT, c.mixF
                wsm, wsmB = (wsS, wsSB) if c.is_s else (wsP, wsPB)
                bst, bstB = (bsS, bsSB) if c.is_s else (bsP, bsPB)

                def spat(e):
                    for hh_ in range(4):
                        ins = e.matmul(ps[3][:, hh_ * 128:(hh_ + 1) * 128], lhsT=vnb[:, hh_ * 128:(hh_ + 1) * 128],
                                       rhs=wsm[:, hh_, :], start=True, stop=True)
                    return ins
                pg.op("pe", spat, reads=[c.vnbB, wsmB], writes=[psB[3]])
                pg.op("dve", lambda e: e.tensor_tensor(out=tmp[:, 0:512], in0=ps[3][:], in1=bst[:], op=ALU.add),
                      reads=[psB[3], bstB], writes=[c.tmpB])
                pg.op("dve", lambda e: e.tensor_tensor(out=mixF[:, 512:1024], in0=tmp[:, 0:512], in1=uT[:, 0:512], op=ALU.mult),
                      reads=[c.tmpB, c.uTB], writes=[c.mixTB])

            def Cout(c):
                mixT, slot = c.mixT, c.slot

                def outproj(e):
                    for k in range(KD):
                        e.matmul(ps[4][:], lhsT=mixT[:, k, :], rhs=wo[:, k, 0:512], start=(k == 0), stop=(k == KD - 1))
                        ins = e.matmul(ps[5][:], lhsT=mixT[:, k, :], rhs=wo[:, k, 512:1024], start=(k == 0), stop=(k == KD - 1))
                    return ins
                pg.op("pe", outproj, reads=[c.mixTB] + woB, writes=[psB[4], psB[5]])
                pg.op("dve", lambda e: e.tensor_tensor(out=xs[:, slot, :], in0=psall[:, 4 * 512:6 * 512], in1=xs[:, slot, :], op=ALU.add),
                      reads=[psB[4], psB[5], xB[slot]], writes=[xB[slot]])

            def ok(i):
                return 0 <= i < n
            for r in range(-2, n + 1):
                t, t1, t2 = r, r + 1, r + 2
                pump(1, 0)
                if ok(t):
                    Bband(C[t])
                if ok(t2):
                    N1a(C[t2])
                if ok(t):
                    Bspat(C[t])
                if ok(t - 1):
                    N2a(C[t - 1])
                if ok(t1):
                    A2pv(C[t1])
                if ok(t):
                    Bpoolw(C[t])
                if ok(t1):
                    A3pv(C[t1])
                if ok(t1):
                    A2u(C[t1])
                if ok(t2):
                    N1b(C[t2])
                if ok(t - 1):
                    N2b(C[t - 1])
                if ok(t):
                    Cout(C[t])

        ffn_state = {"n": 0, "gs": 0, "ae": 0, "hh": 0, "y": 0, "fifo": [], "cs": 0, "wdq": []}

        def ffn_phase(half, l, subtiles, mid_hook=None):
            for j, (c0, gn) in enumerate(GROUPS):
                if j == 5 and mid_hook is not None:
                    mid_hook()
                n = ffn_state["n"]
                s = n % 2
                wg_, wu_, fB = wgs[s], wus[s], fsB[s]
                flush_urgent(n)
                wd_, fdB_ = wds[n % 3], fdB[n % 3]
                last_of_group = None
                for si, (skind, slots) in enumerate(subtiles):
                    is_s = skind == "s"
                    ntok = 128 * len(slots)
                    col0 = slots[0] * 128
                    hi = 1 + ffn_state["hh"] % 3
                    ffn_state["hh"] += 1
                    hh = H[hi][:].rearrange("p (g n) -> p g n", g=G)
                    hhB = HB[hi]
                    want_cs = is_s or (half == 1 and si == len(subtiles) - 1)
                    if want_cs:
                        csi = ffn_state["cs"] % 2
                        ffn_state["cs"] += 1
                    if is_s:
                        pg.dma("sp", f"scs{csi}", [(scs[csi][:, 0:gn * 128], sc_d[l][:, c0 * 128:(c0 + gn) * 128])],
                               writes=[scsB[csi]])

                    gsl = []
                    fifo = ffn_state["fifo"]
                    for ci in range(gn):
                        c = c0 + ci
                        ga, gb = (1, 2) if (ffn_state["gs"] % 2 == 0) else (3, 0)
                        if is_s:
                            ga = gb = 1 if (ffn_state["gs"] % 2 == 0) else 3
                        ffn_state["gs"] += 1
                        uo = 128 if is_s else 0

                        def gu(e, ci=ci, ga=ga, gb=gb, ntok=ntok, col0=col0, wg_=wg_, wu_=wu_, uo=uo, is_s=is_s,
                               csi=(csi if want_cs else 0)):
                            if is_s:
                                e.transpose(ps[ga][:, 256:288], scs[csi][:, ci * 128:(ci + 1) * 128], identf[0:32, 0:32])
                            for k in range(KD):
                                e.matmul(ps[ga][:, 0:ntok], lhsT=wg_[:, k, ci * 128:(ci + 1) * 128],
                                         rhs=h2T[:, k, col0:col0 + ntok], start=(k == 0), stop=(k == KD - 1))
                            for k in range(KD):
                                ins = e.matmul(ps[gb][:, uo:uo + ntok], lhsT=wu_[:, k, ci * 128:(ci + 1) * 128],
                                               rhs=h2T[:, k, col0:col0 + ntok], start=(k == 0), stop=(k == KD - 1))
                            return ins
                        pg.op("pe", gu, reads=fB + [h2B[t] for t in slots] + ([scsB[csi], identfB] if is_s else []),
                              writes=[psB[ga], psB[gb]])
                        if len(fifo) > 1:
                            fifo.pop(0)()
                        pump(1, 1)
                        key = ("gsr", ci)
                        r = ffn_state.get(key, 0)
                        ffn_state[key] = r + 1
                        gst, gstB = S[2 * ci + r % 2], SBf[2 * ci + r % 2]
                        gprev, gprevB = S[2 * ci + (r + 1) % 2], SBf[2 * ci + (r + 1) % 2]
                        ai = 4 + ffn_state["ae"] % 2
                        ei = 6 + ffn_state["ae"] % 2
                        ffn_state["ae"] += 1
                        acc, accB, ge, geB = S[ai], SBf[ai], S[ei], SBf[ei]
                        w0 = cw[:, (l * 3 + 0) * NCH + c:(l * 3 + 0) * NCH + c + 1]
                        w1 = cw[:, (l * 3 + 1) * NCH + c:(l * 3 + 1) * NCH + c + 1]
                        w2 = cw[:, (l * 3 + 2) * NCH + c:(l * 3 + 2) * NCH + c + 1]
                        bb = cb[:, l * NCH + c:l * NCH + c + 1]
                        if is_s:
                            g3 = gst[:, 0:160].rearrange("p (q j) -> p q j", j=10)
                            a3 = acc[:, 0:128].rearrange("p (q j) -> p q j", j=8)
                            gp3 = ps[ga][:, 0:128].rearrange("p (q j) -> p q j", j=8)
                            pg.op("act", lambda e, g3=g3, gp3=gp3: e.activation(out=g3[:, :, 2:10], in_=gp3, func=AF.Copy),
                                  reads=[psB[ga]], writes=[gstB])
                            pg.op("act", lambda e, g3=g3, ci=ci, ga=ga: e.activation(
                                out=g3[:, :, 0:2], in_=ps[ga][:, 256:288].rearrange("p (q r) -> p q r", r=2),
                                func=AF.Copy), reads=[psB[ga]], writes=[gstB])
                            pg.op("act", lambda e, a3=a3, gp3=gp3, w2=w2, bb=bb: e.activation(
                                out=a3, in_=gp3, func=AF.Identity, scale=w2, bias=bb), reads=[psB[ga], parB], writes=[accB])
                            pg.op("dve", lambda e, a3=a3, g3=g3, w1=w1: e.scalar_tensor_tensor(
                                out=a3, in0=g3[:, :, 1:9], scalar=w1, in1=a3, op0=ALU.mult, op1=ALU.add),
                                reads=[gstB, accB, parB], writes=[accB])
                            pg.op("dve", lambda e, a3=a3, g3=g3, w0=w0: e.scalar_tensor_tensor(
                                out=a3, in0=g3[:, :, 0:8], scalar=w0, in1=a3, op0=ALU.mult, op1=ALU.add),
                                reads=[gstB, accB, parB], writes=[accB])
                            pg.op("act", lambda e, g3=g3, ci=ci, csi=csi: e.activation(
                                out=csl[csi][:, ci, :].rearrange("p (q r) -> p q r", r=2), in_=g3[:, :, 8:10], func=AF.Copy),
                                reads=[gstB], writes=[cslB[csi]])
                            pg.op("pe", lambda e, ci=ci, csi=csi, ga=ga: e.transpose(
                                ps[ga][0:32, 320:448], csl[csi][:, ci, :], identf[:]), reads=[cslB[csi], identfB], writes=[psB[ga]])
                            pg.op("act", lambda e, ci=ci, csi=csi, ga=ga: e.activation(
                                out=cso[csi][0:32, ci * 128:(ci + 1) * 128], in_=ps[ga][0:32, 320:448], func=AF.Copy),
                                reads=[psB[ga]], writes=[csoB[csi]])
                        else:
                            pg.op("act", lambda e, gst=gst, ga=ga: e.activation(out=gst[:, 2:514], in_=ps[ga][:], func=AF.Copy),
                                  reads=[psB[ga]], writes=[gstB])
                            if si == 0:
                                if half == 0:
                                    hsrc, hsrcB = cst_t[:, 2:4], cstB
                                else:
                                    hsrc, hsrcB = gcar[:, l, c, 0:2], gcarB[l]
                            else:
                                hsrc, hsrcB = gprev[:, 512:514], gprevB
                            pg.op("act", lambda e, gst=gst, hsrc=hsrc: e.activation(out=gst[:, 0:2], in_=hsrc, func=AF.Copy),
                                  reads=[hsrcB], writes=[gstB])
                            pg.op("act", lambda e, acc=acc, ga=ga, w2=w2, bb=bb: e.activation(
                                out=acc[:, 0:512], in_=ps[ga][:], func=AF.Identity, scale=w2, bias=bb),
                                reads=[psB[ga], parB], writes=[accB])
                            pg.op("dve", lambda e, acc=acc, gst=gst, w1=w1: e.scalar_tensor_tensor(
                                out=acc[:, 0:512], in0=gst[:, 1:513], scalar=w1, in1=acc[:, 0:512], op0=ALU.mult, op1=ALU.add),
                                reads=[gstB, accB, parB], writes=[accB])
                            pg.op("dve", lambda e, acc=acc, gst=gst, w0=w0: e.scalar_tensor_tensor(
                                out=acc[:, 0:512], in0=gst[:, 0:512], scalar=w0, in1=acc[:, 0:512], op0=ALU.mult, op1=ALU.add),
                                reads=[gstB, accB, parB], writes=[accB])
                            if half == 0 and si == len([x for x in subtiles if x[0] == "p"]) - 1:
                                pg.op("act", lambda e, gst=gst, l=l, c=c: e.activation(out=gcar[:, l, c, 0:2], in_=gst[:, 512:514],
                                                                                      func=AF.Copy),
                                      reads=[gstB], writes=[gcarB[l]])
                            if want_cs:
                                pg.dma("sp", f"o_cp{ci}_{r % 2}", [(ncp_o[l][:, c * 128:(c + 1) * 128].rearrange("r p -> p r"), gst[:, 512:514])],
                                       reads=[gstB], is_output=True, slow=True)
                        pg.op("act", lambda e, acc=acc, ge=ge, ntok=ntok: e.activation(out=ge[:, 0:ntok], in_=acc[:, 0:ntok], func=AF.Gelu),
                              reads=[accB], writes=[geB])
                        pg.op("dve", lambda e, ge=ge, hh=hh, ci=ci, gb=gb, ntok=ntok, uo=uo: e.tensor_tensor(
                            out=hh[:, ci, 0:ntok], in0=ge[:, 0:ntok], in1=ps[gb][:, uo:uo + ntok], op=ALU.mult),
                            reads=[geB, psB[gb]], writes=[hhB])
                    if want_cs and is_s:
                        pg.dma("sp", f"o_cs{csi}", [(ncs_o[l][:, c0 * 128:(c0 + gn) * 128], cso[csi][0:32, 0:gn * 128])],
                               reads=[csoB[csi]], is_output=True)

                    def down(sel, slots=slots, hh=hh, hhB=hhB, gn=gn, wd_=wd_, fB=fdB_):
                        for li in sel:
                            slot = slots[li]
                            ya, yb = (4, 5) if ffn_state["y"] % 2 == 0 else (6, 7)
                            ffn_state["y"] += 1

                            def dn(e, li=li, ya=ya, yb=yb):
                                for ci in range(gn):
                                    e.matmul(ps[ya][:], lhsT=hh[:, ci, li * 128:(li + 1) * 128], rhs=wd_[:, ci, 0:512],
                                             start=(ci == 0), stop=(ci == gn - 1))
                                    ins = e.matmul(ps[yb][:], lhsT=hh[:, ci, li * 128:(li + 1) * 128], rhs=wd_[:, ci, 512:1024],
                                                   start=(ci == 0), stop=(ci == gn - 1))
                                return ins
                            pg.op("pe", dn, reads=[hhB, fB], writes=[psB[ya], psB[yb]])
                            pg.op("dve", lambda e, slot=slot, ya=ya: e.tensor_tensor(
                                out=xs[:, slot, :], in0=psall[:, ya * 512:(ya + 2) * 512], in1=xs[:, slot, :], op=ALU.add),
                                reads=[psB[ya], psB[yb], xB[slot]], writes=[xB[slot]])
                    nt = len(slots)
                    hook_n = n if si == len(subtiles) - 1 else None
                    pieces = [list(range(0, (nt + 1) // 2)), list(range((nt + 1) // 2, nt))]
                    pieces = [p_ for p_ in pieces if p_]
                    for pi, p_ in enumerate(pieces):
                        def piece(p_=p_, down=down, last=(pi == len(pieces) - 1), hook_n=hook_n):
                            down(p_)
                            if last and hook_n is not None:
                                if ffn_state["wdq"]:
                                    emit_wd_load(ffn_state["wdq"].pop(0))
                                ffn_state["wdq"].append(hook_n + 3)
                        fifo.append(piece)
                emit_ffn_load(n + 2)
                ffn_state["n"] += 1

        def flush_down():
            fifo = ffn_state["fifo"]
            while fifo:
                fifo.pop(0)()

        fin_state = {"i": 0}

        def final_tile(slot, row0):
            rs, rsB = rms_stats(xs[:, slot, :], xB[slot], D)
            items = []
            bufs = []
            for hf in range(2):
                i = fin_state["i"] % 4
                fin_state["i"] += 1
                pg.op("dve", lambda e, i=i, hf=hf: e.scalar_tensor_tensor(
                    out=S[i][:, 0:512], in0=xs[:, slot, hf * 512:(hf + 1) * 512], scalar=rs, in1=g1bc[:, hf * 512:(hf + 1) * 512],
                    op0=ALU.mult, op1=ALU.mult), reads=[xB[slot], rsB, g1B], writes=[SBf[i]])
                items.append((y_all[row0:row0 + 128, hf * 512:(hf + 1) * 512], S[i][:, 0:512]))
                bufs.append(SBf[i])
            pg.dma("sp", f"o_y{(fin_state['i'] // 2) % 2}", items, reads=bufs, is_output=True)

        halves = [
            ([(i, "p", i) for i in range(8)] + [(8, "s", None)], [("p", [0, 1, 2, 3]), ("p", [4, 5, 6, 7]), ("s", [8])]),
            ([(i, "p", 8 + i) for i in range(8)], [("p", [0, 1, 2, 3]), ("p", [4, 5, 6, 7])]),
        ]
        emit_mixer_load(0)
        emit_ffn_load(0)
        emit_ffn_load(1)
        emit_wd_load(0)
        emit_wd_load(1)
        emit_wd_load(2)
        flush_urgent(10 ** 9)
        flush_bg()
        def load_x(slot, kind, ptile):
            row0 = 2048 if kind == "s" else ptile * 128
            pg.dma("sp", f"x{slot}", [(xs[:, slot, :], x_all[row0:row0 + 128, :])], writes=[xB[slot]])

        for (slot, kind, ptile) in halves[0][0]:
            load_x(slot, kind, ptile)
        emit_layer_tables(0, 0)
        emit_ws_mask(0)
        for half, (tiles, subtiles) in enumerate(halves):
            for l in range(L):
                mixer_phase(half, l, tiles)
                hook = None
                if l + 1 < L:
                    emit_mixer_load(l + 1)
                    emit_layer_tables(l + 1, half)
                    hook = (lambda half=half: emit_ws_mask(half))
                else:
                    pg.dma("sp", "tab", [(g1bc[:], gf_d[0:1, :].partition_broadcast(128))], writes=[g1B])
                    if half == 0:
                        emit_mixer_load(0)
                        emit_layer_tables(0, 1, with_g1=False)
                        hook = (lambda: emit_ws_mask(1))
                ffn_phase(half, l, subtiles, hook)
                flush_down()
                flush_bg()
            for i, (slot, kind, ptile) in enumerate(tiles):
                row0 = 2048 if kind == "s" else ptile * 128
                final_tile(slot, row0)
                if half == 0 and i < len(halves[1][0]):
                    load_x(*halves[1][0][i])
            if half == 0:
                pg.dma("sp", "tab", [(g1bc[:], g1_d[0:1, :].partition_broadcast(128))], writes=[g1B])
        pg.finalize()
    return nc


def _consts():
    t = np.arange(128)
    s = np.arange(128)
    S_, T_ = np.meshgrid(s, t, indexing="ij")
    bands = np.zeros((24, 128, 128), np.float32)
    for g, w in enumerate(WINS):
        d = T_ - S_
        cur = ((d >= 0) & (d < w)).astype(np.float32)
        first = cur.copy()
        gen = cur.copy()
        cnt = np.minimum(t + 1, w).astype(np.float32)
        first[t, t] = 1.0 - cnt
        gen[t, t] = 1.0 - w
        bands[g] = first
        bands[4 + g] = gen
        dp = T_ + 128 - S_
        bands[8 + g] = ((dp >= 0) & (dp < w)).astype(np.float32)
        same = (S_ // 8) == (T_ // 8)
        sc = (same & (d >= 0) & (d < w)).astype(np.float32)
        sc[t, t] = 1.0 - w
        bands[12 + g] = sc
        for a in range(2):
            m = np.zeros((128, 128), np.float32)
            for r in range(120):
                q = a * 8 + r // 15
                j = r % 15
                for tt in range(128):
                    if tt // 8 == q and (tt % 8 + 15 - j) < w:
                        m[r, tt] = 1.0
            bands[16 + 4 * a + g] = m
    bands_h = np.ascontiguousarray(bands.transpose(1, 0, 2))
    maskP = (S_ <= T_).astype(np.float32)
    maskS = (((S_ // 8) == (T_ // 8)) & ((S_ % 8) <= (T_ % 8))).astype(np.float32)
    rc1 = np.stack([1.0 / np.minimum(t + 1, w) for w in WINS]).astype(np.float32).reshape(1, 512)
    return bands_h, maskP, maskS, rc1


_NC_CACHE = {}


def kernel(x_prompt, x_sample, state_pool, state_conv, norm1_g, w_in, pool_w, pool_scale,
           v_norm_g, w_spatial, b_spatial, w_out, norm2_g, w_gate, w_up, conv_w, conv_b,
           w_down, final_norm_g):
    f = lambda a: np.ascontiguousarray(np.asarray(a, dtype=np.float32))
    x_prompt, x_sample, state_pool, state_conv = f(x_prompt), f(x_sample), f(state_pool), f(state_conv)
    w_spatial, b_spatial = f(w_spatial), f(b_spatial)
    bands_h, maskP, maskS, rc1 = _consts()
    wsT = f(w_spatial.transpose(0, 3, 1, 2).reshape(L, 128, 512))
    wsS = f(np.tile(w_spatial[:, :, :8, :8].transpose(0, 3, 1, 2), (1, 16, 1, 16)).reshape(L, 128, 512))
    bsP = f(b_spatial.reshape(L, 512))
    bsS = f(np.tile(b_spatial[:, :, :8], (1, 1, 16)).reshape(L, 512))
    pscT = f(np.asarray(pool_scale, np.float32).reshape(L, 4, 128).transpose(2, 0, 1).reshape(128, L * 4))
    cwT = f(np.asarray(conv_w, np.float32).reshape(L, 3, NCH, 128).transpose(3, 0, 1, 2).reshape(128, L * 3 * NCH))
    cbT = f(np.asarray(conv_b, np.float32).reshape(L, NCH, 128).transpose(2, 0, 1).reshape(128, L * NCH))
    assert NCH % G == 0
    wg_r = f(w_gate).reshape(L, KD, 128, NG, G * 128).transpose(0, 3, 2, 1, 4).reshape(L, NG, 128, G * 1024)
    wu_r = f(w_up).reshape(L, KD, 128, NG, G * 128).transpose(0, 3, 2, 1, 4).reshape(L, NG, 128, G * 1024)
    wd_r = f(w_down).reshape(L, NG, G, 128, D).transpose(0, 1, 3, 2, 4).reshape(L, NG, 128, G * 1024)
    wgu_h = np.ascontiguousarray(np.concatenate([wg_r, wu_r], axis=3))
    wd_h = np.ascontiguousarray(wd_r)
    wi_h = f(f(w_in).reshape(L, KD, 128, WIN).transpose(0, 2, 1, 3).reshape(L, 128, KD * WIN))
    wo_h = f(f(w_out).reshape(L, KD, 128, D).transpose(0, 2, 1, 3).reshape(L, 128, KD * D))
    pw_h = f(f(pool_w).transpose(0, 2, 1, 3).reshape(L, 128, 512))
    shared = {
        "wi_h": wi_h, "wo_h": wo_h, "wgu_h": wgu_h, "wd_h": wd_h, "pw_h": pw_h, "wsT": wsT, "wsS": wsS, "g1": f(norm1_g), "g2": f(norm2_g),
        "gf": f(final_norm_g).reshape(1, D), "vg": f(v_norm_g), "bsP": bsP, "bsS": bsS,
        "pscT": pscT, "cwT": cwT, "cbT": cbT, "identf": np.eye(128, dtype=np.float32),
        "maskP": maskP, "maskS": maskS, "bands": bands_h, "rc1": rc1,
    }
    in_maps = []
    for c in range(8):
        m = dict(shared)
        m["x_all"] = f(np.concatenate([x_prompt[c], x_sample[16 * c:16 * c + 16].reshape(128, D)], axis=0))
        m["sp"] = f(state_pool[:, 16 * c:16 * c + 16].reshape(L, 240, 512))
        m["sc"] = f(state_conv[:, 16 * c:16 * c + 16].reshape(L, 32, FF))
        in_maps.append(m)
    if "nc" not in _NC_CACHE:
        _NC_CACHE["nc"] = build_program()
    res = run_bass_kernel_spmd(_NC_CACHE["nc"], in_maps, core_ids=list(range(8)))
    R = res.results
    y_prompt = np.stack([R[c]["y_all"][:2048] for c in range(8)]).astype(np.float32)
    y_sample = np.concatenate([R[c]["y_all"][2048:].reshape(16, 8, D) for c in range(8)], axis=0).astype(np.float32)
    npp = np.stack([R[c]["npp"] for c in range(8)], axis=1).astype(np.float32)
    nps = np.concatenate([R[c]["nps"] for c in range(8)], axis=1).astype(np.float32)
    ncp = np.stack([R[c]["ncp"] for c in range(8)], axis=1).astype(np.float32)
    ncs = np.concatenate([R[c]["ncs"].reshape(L, 16, 2, FF) for c in range(8)], axis=1).astype(np.float32)
    nvs = np.concatenate([R[c]["nvs"].reshape(L, 16, 8, 512) for c in range(8)], axis=1).astype(np.float32)
    return (y_prompt, y_sample, npp, nps, ncp, ncs, nvs)
```
